# Optimizing a Trainium2 kernel written in Bass

```python
import jax, jax.numpy as jnp
from jax import lax
import numpy as np

D_MODEL = 2048
BATCH = 4
SEQ = 2048
DEPTH = 4
DEC_BATCH = 1
DEC_SEQ = 16384
PAST_LEN = 128

N_MIXERS = 2
N_POOL_LAYERS = (DEPTH + 1) // 2
N_ATTN_LAYERS = DEPTH // 2
POOL_WINDOWS = (2, 4, 8, 16)
N_POOL_GROUPS = len(POOL_WINDOWS)
POOL_GROUP = D_MODEL // N_POOL_GROUPS
HEAD_DIM = 64
N_HEADS = D_MODEL // HEAD_DIM
N_KV_HEADS = N_HEADS // 8
GQA_GROUP = N_HEADS // N_KV_HEADS
WINDOW = 128
BLOCK = 128
D_FF = 5632
CONV_WIDTH = 3
EPS = 1e-6

kernel_name = "hybrid_pool_swa_convffn_encoder"


def rmsnorm(x, g):
    xf = x.astype(jnp.float32)
    y = xf * lax.rsqrt(jnp.mean(xf * xf, axis=-1, keepdims=True) + EPS)
    return (y * g.astype(jnp.float32)).astype(x.dtype)


def alibi_slopes():
    h = np.arange(1, N_HEADS + 1, dtype=np.float32)
    return jnp.asarray(2.0 ** (-8.0 * h / N_HEADS), dtype=jnp.float32)


def pool_mixer(h, w, scale):
    B, S, D = h.shape
    hf = h.astype(jnp.float32)
    cs = jnp.concatenate([jnp.zeros((B, 1, D), jnp.float32), jnp.cumsum(hf, axis=1)], axis=1)
    t = jnp.arange(S)
    outs = []
    for g, win in enumerate(POOL_WINDOWS):
        lo = jnp.clip(t - win // 2, 0, S)
        hi = jnp.clip(t + win // 2, 0, S)
        sl = slice(g * POOL_GROUP, (g + 1) * POOL_GROUP)
        csg = cs[:, :, sl]
        cnt = (hi - lo).astype(jnp.float32)[None, :, None]
        mean = (csg[:, hi] - csg[:, lo]) / cnt
        outs.append(mean - hf[:, :, sl])
    p = jnp.stack(outs, axis=2).astype(h.dtype)
    y = jnp.einsum('bsgc,gcd->bsgd', p, w).reshape(B, S, D)
    return y * scale


def window_attention(h, w_qkv, w_o, sink):
    B, S, D = h.shape
    nb = S // BLOCK
    qkv = h @ w_qkv
    q, k, v = jnp.split(qkv, [N_HEADS * HEAD_DIM, (N_HEADS + N_KV_HEADS) * HEAD_DIM], axis=-1)
    q = q.reshape(B, nb, BLOCK, N_KV_HEADS, GQA_GROUP, HEAD_DIM)

    def band(z):
        z = z.reshape(B, S, N_KV_HEADS, HEAD_DIM)
        z = jnp.pad(z, ((0, 0), (BLOCK, BLOCK), (0, 0), (0, 0)))
        z = z.reshape(B, nb + 2, BLOCK, N_KV_HEADS, HEAD_DIM)
        return jnp.concatenate([z[:, :-2], z[:, 1:-1], z[:, 2:]], axis=2)

    kb, vb = band(k), band(v)
    s = jnp.einsum('bnqkgd,bnckd->bnkgqc', q, kb).astype(jnp.float32) * (HEAD_DIM ** -0.5)

    qi = jnp.arange(BLOCK)[:, None]
    kc = jnp.arange(3 * BLOCK)[None, :]
    rel = kc - BLOCK - qi
    kpos = jnp.arange(nb)[:, None] * BLOCK + jnp.arange(3 * BLOCK)[None, :] - BLOCK
    valid = (jnp.abs(rel) <= WINDOW)[None] & ((kpos >= 0) & (kpos < S))[:, None, :]
    bias = -alibi_slopes()[:, None, None] * jnp.abs(rel).astype(jnp.float32)[None]
    bias = bias.reshape(N_KV_HEADS, GQA_GROUP, BLOCK, 3 * BLOCK)
    s = jnp.where(valid[None, :, None, None], s + bias[None, None], -jnp.inf)

    sk = sink.astype(jnp.float32).reshape(1, 1, N_KV_HEADS, GQA_GROUP, 1, 1)
    m = jnp.maximum(jnp.max(s, axis=-1, keepdims=True), sk)
    e = jnp.exp(s - m)
    p = e / (jnp.sum(e, axis=-1, keepdims=True) + jnp.exp(sk - m))
    o = jnp.einsum('bnkgqc,bnckd->bnqkgd', p.astype(vb.dtype), vb).reshape(B, S, N_HEADS * HEAD_DIM)
    return o @ w_o


def conv_ffn(h, w_up, conv_w, conv_b, w_down):
    u = h @ w_up
    up = jnp.pad(u, ((0, 0), (1, 1), (0, 0)))
    u = up[:, :-2] * conv_w[0] + up[:, 1:-1] * conv_w[1] + up[:, 2:] * conv_w[2] + conv_b
    g, val = jnp.split(u, 2, axis=-1)
    return (jax.nn.silu(g) * val) @ w_down


def trunk(x, norm_mix, norm_ffn, norm_final, pool_w, pool_scale, attn_wqkv, attn_wo, attn_sink,
          ffn_wup, ffn_conv_w, ffn_conv_b, ffn_wdown):
    for i in range(DEPTH):
        h = rmsnorm(x, norm_mix[i])
        j = i // N_MIXERS
        if i % N_MIXERS == 0:
            x = x + pool_mixer(h, pool_w[j], pool_scale[j])
        else:
            x = x + window_attention(h, attn_wqkv[j], attn_wo[j], attn_sink[j])
        x = x + conv_ffn(rmsnorm(x, norm_ffn[i]), ffn_wup[i], ffn_conv_w[i], ffn_conv_b[i], ffn_wdown[i])
    return rmsnorm(x, norm_final)


def setup_inputs(seed: int = 0) -> dict:
    key = jax.random.key(seed)
    ks = jax.random.split(key, 16)
    f32 = jnp.float32
    nrm = lambda k, shape, s: jax.random.normal(k, shape, f32) * s
    QKV = (N_HEADS + 2 * N_KV_HEADS) * HEAD_DIM
    center = jnp.zeros((CONV_WIDTH, 1), f32).at[CONV_WIDTH // 2].set(1.0)
    return {
        "x_prompt": nrm(ks[0], (BATCH, SEQ, D_MODEL), 1.0),
        "x_sample": nrm(ks[1], (DEC_BATCH, DEC_SEQ, D_MODEL), 1.0),
        "norm_mix": 1.0 + nrm(ks[2], (DEPTH, D_MODEL), 0.02),
        "norm_ffn": 1.0 + nrm(ks[3], (DEPTH, D_MODEL), 0.02),
        "norm_final": 1.0 + nrm(ks[4], (D_MODEL,), 0.02),
        "pool_w": nrm(ks[5], (N_POOL_LAYERS, N_POOL_GROUPS, POOL_GROUP, POOL_GROUP), POOL_GROUP ** -0.5),
        "pool_scale": 1.0 + nrm(ks[6], (N_POOL_LAYERS, D_MODEL), 0.02),
        "attn_wqkv": nrm(ks[7], (N_ATTN_LAYERS, D_MODEL, QKV), D_MODEL ** -0.5),
        "attn_wo": nrm(ks[8], (N_ATTN_LAYERS, N_HEADS * HEAD_DIM, D_MODEL), (N_HEADS * HEAD_DIM) ** -0.5),
        "attn_sink": nrm(ks[9], (N_ATTN_LAYERS, N_HEADS), 0.5),
        "ffn_wup": nrm(ks[10], (DEPTH, D_MODEL, 2 * D_FF), D_MODEL ** -0.5),
        "ffn_conv_w": center[None] + nrm(ks[11], (DEPTH, CONV_WIDTH, 2 * D_FF), 0.2),
        "ffn_conv_b": nrm(ks[12], (DEPTH, 2 * D_FF), 0.01),
        "ffn_wdown": nrm(ks[13], (DEPTH, D_FF, D_MODEL), D_FF ** -0.5),
    }


def reference(x_prompt, x_sample, norm_mix, norm_ffn, norm_final, pool_w, pool_scale, attn_wqkv, attn_wo,
              attn_sink, ffn_wup, ffn_conv_w, ffn_conv_b, ffn_wdown):
    y_prompt = trunk(x_prompt, norm_mix, norm_ffn, norm_final, pool_w, pool_scale, attn_wqkv, attn_wo,
                     attn_sink, ffn_wup, ffn_conv_w, ffn_conv_b, ffn_wdown)
    y_sample = trunk(x_sample, norm_mix, norm_ffn, norm_final, pool_w, pool_scale, attn_wqkv, attn_wo,
                     attn_sink, ffn_wup, ffn_conv_w, ffn_conv_b, ffn_wdown)
    return (y_prompt, y_sample)
```

```python
import numpy as np
import ml_dtypes
from contextlib import ExitStack
import concourse.bass as bass
import concourse.mybir as mybir
from concourse.bass_utils import run_bass_kernel_spmd

F32 = mybir.dt.float32
BF16 = mybir.dt.bfloat16
AF = mybir.ActivationFunctionType
ALU = mybir.AluOpType
AX = mybir.AxisListType
NPBF = ml_dtypes.bfloat16

D = 2048
DFF = 5632
NCH = 16
NFC = 44
NGRP = 4
GP = 11
NHEAD = 32
EPS = 1e-6
NCORES = 8
HB = 3
SEGS = (dict(s0=128, nblk=22, own0=512, own1=2560, nown=2048),
        dict(s0=2944, nblk=14, own0=3328, own1=4352, nown=1024))
NB = 38
NTOK = NB * 128
NOUT = 3072
R_SLOTS = 16
TILE_OUT = 510
POOL_WINDOWS = (2, 4, 8, 16)

U_POOL = {}
U_ATTN = {}
U_FFN = {}
_u = 0
for _i in range(4):
    if _i % 2 == 0:
        U_POOL[_i // 2] = _u
        _u += 4
    else:
        U_ATTN[_i // 2] = _u
        _u += 40
    U_FFN[_i] = _u
    _u += 132
NU = _u
CONV_CHUNK = 4

PHASES = (("pool", 0, 0, 3), ("ffn", 0, 0, 267), ("attn", 1, 0, 2), ("ffn", 1, 1, 138),
          ("pool", 2, 1, 2), ("ffn", 2, 2, 129), ("attn", 3, 1, 1), ("ffn", 3, 3, 0))

ENGS = ("pe", "act", "dve", "pool", "sp")
EPOCH = 16000
NDMASEM = 24


class Buf:
    __slots__ = ("name", "w", "r")

    def __init__(self, name=""):
        self.name = name
        self.w = None
        self.r = []


class Op:
    __slots__ = ("eng", "fn", "deps", "sig", "tok", "dma", "k")

    def __init__(self, eng, fn, dma):
        self.eng = eng
        self.fn = fn
        self.dma = dma
        self.deps = []
        self.sig = dma
        self.tok = None
        self.k = None


class Prog:
    def __init__(self):
        self.ops = {e: [] for e in ENGS}
        self.ndma = {e: 0 for e in ENGS}

    def add(self, eng, fn, reads=(), writes=(), dma=False):
        op = Op(eng, fn, dma)
        raw = set()
        deps = set()
        for b in reads:
            if b.w is not None:
                raw.add(b.w)
                deps.add(b.w)
        for b in writes:
            if b.w is not None:
                deps.add(b.w)
            deps.update(b.r)
        for d in deps:
            if d is op:
                continue
            if (not d.dma) and (not dma) and d.eng == eng:
                if eng == "pe" or d not in raw:
                    continue
            d.sig = True
            op.deps.append(d)
        for b in reads:
            b.r.append(op)
        for b in writes:
            b.w = op
            b.r = []
        if dma:
            op.k = self.ndma[eng]
            self.ndma[eng] += 1
        self.ops[eng].append(op)
        return op

    def barrier(self, engs=("pe", "act", "dve", "pool")):
        last = {}
        for e in engs:
            lst = self.ops[e]
            j = len(lst) - 1
            while j >= 0 and lst[j].fn is None:
                j -= 1
            last[e] = lst[j] if j >= 0 else None
        dmas = [o for o in self.ops["pool"] if o.dma][-NDMASEM:]
        for e in engs:
            m = Op(e, None, False)
            for e2 in engs:
                if e2 != e and last[e2] is not None:
                    last[e2].sig = True
                    m.deps.append(last[e2])
            m.deps.extend(dmas)
            self.ops[e].append(m)

    def emit(self, nc, stack, final_waits):
        ctr = {}
        dsem = {}
        for e in ENGS:
            nsig = sum(1 for o in self.ops[e] if o.sig and not o.dma)
            ctr[e] = [stack.enter_context(nc.semaphore(f"c_{e}_{i}")) for i in range((nsig + EPOCH - 1) // EPOCH)]
            if self.ndma[e]:
                dsem[e] = [stack.enter_context(nc.semaphore(f"d_{e}_{i}")) for i in range(NDMASEM)]
        for e in ENGS:
            i = 0
            for o in self.ops[e]:
                if o.dma:
                    o.tok = (dsem[e][o.k % NDMASEM], 16 * (o.k // NDMASEM + 1))
                elif o.sig:
                    o.tok = (ctr[e][i // EPOCH], i % EPOCH + 1)
                    i += 1
        block = stack.enter_context(nc.Block())
        prog = self

        def run(eng_name, h):
            waited = {}

            def w(tok):
                s, v = tok
                key = id(s)
                if waited.get(key, 0) < v:
                    h.wait_ge(s, v)
                    waited[key] = v

            for o in prog.ops[eng_name]:
                for d in o.deps:
                    w(d.tok)
                if o.dma and o.k >= NDMASEM:
                    w((o.tok[0], o.tok[1] - 16))
                if o.fn is None:
                    continue
                inst = o.fn(h)
                if o.sig:
                    inst.then_inc(o.tok[0], 16 if o.dma else 1)
            for o in final_waits.get(eng_name, ()):
                w(o.tok)

        @block.tensor
        def _(h):
            run("pe", h)

        @block.scalar
        def _(h):
            run("act", h)

        @block.vector
        def _(h):
            run("dve", h)

        @block.gpsimd
        def _(h):
            run("pool", h)

        @block.sync
        def _(h):
            run("sp", h)


NU_AT = (4, 136, 176, 308, 312, 444, 484, 616)


def build_program(nphases=8, debug=False):
    nc = bass.Bass("TRN2", target_bir_lowering=False)
    P = Prog()
    NUE = NU_AT[nphases - 1]

    def din(name, shape, dt=F32):
        return nc.dram_tensor(name, list(shape), dt, kind="ExternalInput").ap()

    x_in = din("x_in", [NTOK, D])
    vmask = din("vmask", [NTOK, 1])
    kmask_d = din("kmask", [1, NTOK])
    wsrc = din("wsrc", [NUE, 128, 2048])
    gmix = din("gmix", [4, D])
    gffn = din("gffn", [4, D])
    gfin = din("gfin", [1, D])
    pscale = din("pscale", [2, D])
    sink_d = din("sink", [2, NHEAD])
    cp_d = din("cp", [4, 128, 88 * 4])
    base_d = din("base", [128, 384])
    wband_d = din("wband", [128, 12 * 128], BF16)
    ident_d = din("ident", [128, 128], BF16)
    y_out = nc.dram_tensor("y", [NOUT, D], F32, kind="ExternalOutput").ap()
    WPART = 156
    wbf_parts = [nc.dram_tensor(f"wbf{i}", [WPART, 128, 2048], BF16, kind="Internal").ap()
                 for i in range((NU + WPART - 1) // WPART)]
    xa = nc.dram_tensor("xa", [NTOK, D], F32, kind="Internal").ap()
    xb = nc.dram_tensor("xb", [NTOK, D], F32, kind="Internal").ap()
    dbg = nc.dram_tensor("dbg", [NTOK, D], F32, kind="ExternalOutput").ap() if debug else None

    with ExitStack() as gst:
        def gsb(name, shape, dt):
            return gst.enter_context(nc.sbuf_tensor(name, list(shape), dt))

        wring = gsb("wring", [128, R_SLOTS, 2048], BF16)
        ident = gsb("ident_sb", [128, 128], BF16)
        mhalf = gsb("mhalf", [128, 8], F32)
        zero_t = gsb("zero_t", [128, 2048], BF16)
        ps = gst.enter_context(nc.psum_tensor("ps", [128, 4096], F32))

        ringbuf = [Buf(f"ring{i}") for i in range(R_SLOTS)]
        psb = [Buf(f"psb{i}") for i in range(8)]
        convbuf = [Buf(f"conv{i}") for i in range((NU + CONV_CHUNK - 1) // CONV_CHUNK)]
        b_ident = Buf("ident")
        b_mhalf = Buf("mhalf")
        b_zero = Buf("zero")
        dbuf = {"x_in": [Buf() for _ in range(NB)], "xa": [Buf() for _ in range(NB)],
                "xb": [Buf() for _ in range(NB)], "y": [Buf() for _ in range(NOUT // 128)]}
        dap = {"x_in": x_in, "xa": xa, "xb": xb, "y": y_out}

        def bank(i, n=512, off=0):
            return ps[:, i * 512 + off: i * 512 + off + n]

        def bank_bf(i):
            return ps[:, i * 512:(i + 1) * 512].bitcast(BF16)

        def blocks_of(r0, n):
            return list(range(r0 // 128, (r0 + n - 1) // 128 + 1))

        def dma(q, out, in_, reads, writes):
            return P.add(q, lambda e: e.dma_start(out=out, in_=in_), reads, writes, dma=True)

        def load_rows(q, dst_ap, name, r0, m, writes):
            return dma(q, dst_ap, dap[name][r0:r0 + m, :], [dbuf[name][b] for b in blocks_of(r0, m)], writes)

        out_ops = []

        def store_rows(q, name, r0, m, src_ap, reads, yrow=None):
            if name == "y":
                o = dma(q, y_out[yrow:yrow + m, :], src_ap, reads, [dbuf["y"][b] for b in blocks_of(yrow, m)])
                out_ops.append(o)
                return o
            return dma(q, dap[name][r0:r0 + m, :], src_ap, reads, [dbuf[name][b] for b in blocks_of(r0, m)])

        def act(out, in_, func, reads, writes, bias=None, scale=None, accum=None):
            kw = {}
            if bias is not None:
                kw["bias"] = bias
            if scale is not None:
                kw["scale"] = scale
            if accum is not None:
                kw["accum_out"] = accum
            return P.add("act", lambda e: e.activation(out=out, in_=in_, func=func, **kw), reads, writes)

        def stt(eng, out, in0, scalar, in1, op0, op1, reads, writes):
            return P.add(eng, lambda e: e.scalar_tensor_tensor(out=out, in0=in0, scalar=scalar, in1=in1, op0=op0, op1=op1),
                         reads, writes)

        def ts(eng, out, in0, s1, s2, op0, op1, reads, writes):
            if s2 is None:
                return P.add(eng, lambda e: e.tensor_scalar(out=out, in0=in0, scalar1=s1, scalar2=None, op0=op0), reads, writes)
            return P.add(eng, lambda e: e.tensor_scalar(out=out, in0=in0, scalar1=s1, scalar2=s2, op0=op0, op1=op1), reads, writes)

        def tt(eng, out, in0, in1, op, reads, writes):
            return P.add(eng, lambda e: e.tensor_tensor(out=out, in0=in0, in1=in1, op=op), reads, writes)

        def cp_(eng, out, in_, reads, writes):
            return P.add(eng, lambda e: e.tensor_copy(out=out, in_=in_), reads, writes)

        wstate = {"step": 0}

        def wload(uid):
            r = wstate["step"] % R_SLOTS
            wstate["step"] += 1
            dma("sp", wring[:, r, :], wbf_parts[uid // WPART][uid % WPART], [convbuf[uid // CONV_CHUNK]], [ringbuf[r]])
            return r

        conv_state = {"next": 0}

        def emit_conversions(upto_uid):
            upto = min((min(upto_uid, NUE) + CONV_CHUNK - 1) // CONV_CHUNK, len(convbuf))
            while conv_state["next"] < upto:
                k = conv_state["next"]
                lo, hi = k * CONV_CHUNK, min((k + 1) * CONV_CHUNK, NUE)
                dma("pool", wbf_parts[lo // WPART][lo % WPART: lo % WPART + (hi - lo)], wsrc[lo:hi], [], [convbuf[k]])
                conv_state["next"] += 1

        def conv_tick(n=1):
            for _ in range(n):
                if conv_state["next"] < len(convbuf) and conv_state["next"] * CONV_CHUNK < NUE:
                    emit_conversions((conv_state["next"] + 1) * CONV_CHUNK)

        zstate = {"next": 0}

        def zero_tick(n=1):
            for _ in range(n):
                b = zstate["next"]
                if b < NB:
                    dma("pool", xb[b * 128:(b + 1) * 128, :], zero_t[:], [b_zero], [dbuf["xb"][b]])
                    zstate["next"] += 1

        dma("pool", ident[:], ident_d[:, :], [], [b_ident])
        P.add("pool", lambda e: e.memset(mhalf[:], -0.5), [], [b_mhalf])
        P.add("pool", lambda e: e.memset(zero_t[:], 0.0), [], [b_zero])
        emit_conversions(U_POOL[0] + 4)

        def norm_stats(xn_ap, m, ss_ap, junk_ap, b_xn, b_junk, b_ss):
            act(junk_ap[:m, :], xn_ap[:m, :], AF.Square, [b_xn], [b_junk, b_ss], accum=ss_ap[:m, :])

        def norm_rstd(ss_ap, rt_ap, rstd_ap, ncol, b_ss, b_rt, b_rstd):
            ts("dve", rt_ap[:, :ncol], ss_ap[:, :ncol], 1.0 / D, EPS, ALU.mult, ALU.add, [b_ss], [b_rt])
            tt("pool", rstd_ap[:, :ncol], rt_ap[:, :ncol], mhalf[:, :ncol], ALU.pow, [b_rt, b_mhalf], [b_rstd])

        def transposes(hb_ap, m, hT_ap3, col0, banks, b_hb, b_hT):
            for half in range(2):
                bk = banks[half % len(banks)]
                pT = bank_bf(bk)

                def f(e, half=half, pT=pT):
                    i = None
                    for c in range(8):
                        cc = half * 8 + c
                        i = e.transpose(out=pT[:, c * 128:c * 128 + m], in_=hb_ap[:m, cc * 128:(cc + 1) * 128],
                                        identity=ident[:m, :m])
                    return i
                P.add("pe", f, [b_hb, b_ident], [psb[bk]])
                src = pT.rearrange("p (c t) -> p c t", c=8)[:, :, :m]
                act(hT_ap3[:, half * 8:(half + 1) * 8, col0:col0 + m], src, AF.Copy, [psb[bk]], [b_hT])

        def pool_phase(layer, j, halo_blk, src, dst):
            with ExitStack() as st:
                def sb(name, shape, dt):
                    return st.enter_context(nc.sbuf_tensor(f"p{layer}_{name}", list(shape), dt))
                NX = 5
                xn = [sb(f"xn{i}", [128, D], F32) for i in range(NX)]
                hb = [sb(f"hb{i}", [128, D], BF16) for i in range(NX)]
                vb = [sb(f"vb{i}", [128, 128], BF16) for i in range(NX)]
                vm = [sb(f"vm{i}", [128, 1], F32) for i in range(NX)]
                gbc = sb("gbc", [128, D], F32)
                psc = sb("psc", [128, D], F32)
                wband = sb("wband", [128, 12, 128], BF16)
                ones128 = sb("ones128", [128, 128], BF16)
                Mt = [sb(f"M{i}", [128, 12, 128], BF16) for i in range(2)]
                rcnt = sb("rcnt", [128, 4, 128], F32)
                pT = [sb(f"pT{i}", [128, 16, 128], BF16) for i in range(2)]
                t1 = [sb(f"t1{i}", [128, D], F32) for i in range(2)]
                xo = [sb(f"xo{i}", [128, D], F32) for i in range(2)]
                ss = sb("ss", [128, NX], F32)
                rt = sb("rt", [128, NX], F32)
                rstd = sb("rstd", [128, NX], F32)
                b_xn = [Buf() for _ in range(NX)]
                b_hb = [Buf() for _ in range(NX)]
                b_vb = [Buf() for _ in range(NX)]
                b_vm = [Buf() for _ in range(NX)]
                b_ss = [Buf() for _ in range(NX)]
                b_rt = [Buf() for _ in range(NX)]
                b_rstd = [Buf() for _ in range(NX)]
                b_gbc, b_psc, b_wband, b_ones, b_rcnt = Buf(), Buf(), Buf(), Buf(), Buf()
                b_M = [Buf(), Buf()]
                b_pT = [Buf(), Buf()]
                b_t1 = [Buf(), Buf()]
                b_xo = [Buf(), Buf()]

                dma("pool", gbc[:], gmix[layer:layer + 1, :].partition_broadcast(128).squeeze(1), [], [b_gbc])
                dma("pool", psc[:], pscale[j:j + 1, :].partition_broadcast(128).squeeze(1), [], [b_psc])
                dma("pool", wband[:].rearrange("p a t -> p (a t)"), wband_d[:, :], [], [b_wband])
                P.add("pool", lambda e: e.memset(ones128[:], 1.0), [], [b_ones])

                def load_block(bb):
                    i = bb % NX
                    load_rows("pool", xn[i][:], src, bb * 128, 128, [b_xn[i]])
                    dma("pool", vm[i][:], vmask[bb * 128:(bb + 1) * 128, :], [], [b_vm[i]])

                def norm_block(bb):
                    i = bb % NX
                    norm_stats(xn[i], 128, ss[:, i:i + 1], hb[i], b_xn[i], b_hb[i], b_ss[i])
                    ts("dve", rt[:, i:i + 1], ss[:, i:i + 1], 1.0 / D, EPS, ALU.mult, ALU.add, [b_ss[i]], [b_rt[i]])
                    tt("pool", rstd[:, i:i + 1], rt[:, i:i + 1], mhalf[:, 0:1], ALU.pow, [b_rt[i], b_mhalf], [b_rstd[i]])
                    stt("dve", hb[i][:], xn[i][:], rstd[:, i:i + 1], gbc[:], ALU.mult, ALU.mult,
                        [b_xn[i], b_rstd[i], b_gbc], [b_hb[i]])
                    ts("dve", vb[i][:], ones128[:], vm[i][:, 0:1], None, ALU.mult, None, [b_ones, b_vm[i]], [b_vb[i]])

                cnt = {"n": 0}

                def pool_stage1a(b):
                    k = b % 2
                    nb3 = [(b - 1) % NX, b % NX, (b + 1) % NX]

                    def fc(e):
                        i = None
                        for g in range(4):
                            for r in range(3):
                                i = e.matmul(bank(7, 128, g * 128), lhsT=vb[nb3[r]][:], rhs=wband[:, g * 3 + r, :],
                                             start=(r == 0), stop=(r == 2))
                        return i
                    P.add("pe", fc, [b_vb[x] for x in nb3] + [b_wband], [psb[7]])
                    ts("dve", rcnt[:].rearrange("p g t -> p (g t)"), bank(7), 1.0, None, ALU.max, None, [psb[7]], [b_rcnt])
                    P.add("dve", lambda e: e.reciprocal(out=rcnt[:], in_=rcnt[:]), [b_rcnt], [b_rcnt])
                    M4 = Mt[k][:].rearrange("p (g r) t -> p g r t", g=4)
                    W4 = wband[:].rearrange("p (g r) t -> p g r t", g=4)
                    tt("dve", M4, W4, rcnt[:].unsqueeze(2).to_broadcast([128, 4, 3, 128]), ALU.mult,
                       [b_wband, b_rcnt], [b_M[k]])
                    tt("dve", M4[:, :, 1, :], M4[:, :, 1, :], ident[:].unsqueeze(1).to_broadcast([128, 4, 128]),
                       ALU.subtract, [b_M[k], b_ident], [b_M[k]])

                def pool_stage1b(b):
                    k = b % 2
                    nb3 = [(b - 1) % NX, b % NX, (b + 1) % NX]
                    for g in range(4):
                        def fp(e, g=g):
                            i = None
                            for cc in range(4):
                                for r in range(3):
                                    i = e.matmul(bank(g, 128, cc * 128),
                                                 lhsT=hb[nb3[r]][:, g * 512 + cc * 128: g * 512 + (cc + 1) * 128],
                                                 rhs=Mt[k][:, g * 3 + r, :], start=(r == 0), stop=(r == 2))
                            return i
                        P.add("pe", fp, [b_hb[x] for x in nb3] + [b_M[k]], [psb[g]])
                        act(pT[k][:, g * 4:(g + 1) * 4, :].rearrange("p c t -> p (c t)"), bank(g), AF.Copy, [psb[g]], [b_pT[k]])

                def pool_stage2(b):
                    k = b % 2
                    for g in range(4):
                        rs = wload(U_POOL[j] + g)

                        yb = 4 + cnt["n"] % 3
                        cnt["n"] += 1

                        def fy(e, g=g, rs=rs, yb=yb):
                            i = None
                            for cc in range(4):
                                i = e.matmul(bank(yb), lhsT=pT[k][:, g * 4 + cc, :], rhs=wring[:, rs, cc * 512:(cc + 1) * 512],
                                             start=(cc == 0), stop=(cc == 3))
                            return i
                        P.add("pe", fy, [b_pT[k], ringbuf[rs]], [psb[yb]])
                        stt("dve", t1[k][:, g * 512:(g + 1) * 512], bank(yb), vm[b % NX][:, 0:1], psc[:, g * 512:(g + 1) * 512],
                            ALU.mult, ALU.mult, [psb[yb], b_vm[b % NX], b_psc], [b_t1[k]])
                    tt("dve", xo[k][:], t1[k][:], xn[b % NX][:], ALU.add, [b_t1[k], b_xn[b % NX]], [b_xo[k]])
                    store_rows("pool", dst, b * 128, 128, xo[k][:], [b_xo[k]])

                for seg in SEGS:
                    b0 = seg["own0"] // 128 - halo_blk
                    b1 = seg["own1"] // 128 + halo_blk
                    load_block(b0 - 1)
                    for bb in range(b0 - 1, b1 + 4):
                        if b0 <= bb - 4 < b1:
                            pool_stage2(bb - 4)
                        if bb + 1 <= b1:
                            load_block(bb + 1)
                        if b0 <= bb - 3 < b1:
                            pool_stage1b(bb - 3)
                        if b0 <= bb - 2 < b1:
                            pool_stage1a(bb - 2)
                        if bb <= b1:
                            norm_block(bb)
                        conv_tick(1)
                        zero_tick(1)
                P.barrier()

        def ffn_phase(layer, halo, src, dst, final):
            with ExitStack() as st:
                def sb(name, shape, dt):
                    return st.enter_context(nc.sbuf_tensor(f"f{layer}_{name}", list(shape), dt))
                actT = [sb(f"actT{i}", [128, GP, 512], BF16) for i in range(2)]
                hT = [sb(f"hT{i}", [128, NCH, 512], BF16) for i in range(2)]
                NXN = 2
                xn = [sb(f"xn{i}", [128, D], F32) for i in range(NXN)]
                hb = [sb(f"hb{i}", [128, D], BF16) for i in range(1)]
                oacc = [sb(f"oacc{i}", [128, D], F32) for i in range(4)]
                yg = [sb(f"yg{i}", [128, 512], F32) for i in range(2)]
                yv = [sb(f"yv{i}", [128, 512], F32) for i in range(2)]
                sg = [sb(f"sg{i}", [128, 512], F32) for i in range(2)]
                gbc = sb("gbc", [128, D], F32)
                cpt = sb("cp", [128, 88 * 4], F32)
                ss = [sb(f"ss{i}", [128, 4], F32) for i in range(2)]
                rt = [sb(f"rt{i}", [128, 4], F32) for i in range(2)]
                rstd = [sb(f"rstd{i}", [128, 4], F32) for i in range(2)]
                vm = [sb(f"vm{i}", [128, 4], F32) for i in range(2)]
                b_actT = [Buf(), Buf()]
                b_hT = [Buf(), Buf()]
                b_xn = [Buf() for _ in range(NXN)]
                b_hb = [Buf()]
                b_oacc = [[Buf() for _ in range(4)] for _ in range(4)]
                b_yg = [Buf(), Buf()]
                b_yv = [Buf(), Buf()]
                b_sg = [Buf(), Buf()]
                b_gbc, b_cp = Buf(), Buf()
                b_ss = [Buf(), Buf()]
                b_rt = [Buf(), Buf()]
                b_rstd = [Buf(), Buf()]
                b_vm = [Buf(), Buf()]
                if final:
                    gfb = sb("gfb", [128, D], F32)
                    ss2 = sb("ss2", [128, 4], F32)
                    rt2 = sb("rt2", [128, 4], F32)
                    rstd2 = sb("rstd2", [128, 4], F32)
                    b_gfb, b_ss2, b_rt2, b_rstd2 = Buf(), Buf(), Buf(), Buf()
                    dma("pool", gfb[:], gfin[0:1, :].partition_broadcast(128).squeeze(1), [], [b_gfb])
                for s_ in ss:
                    P.add("pool", lambda e, s_=s_: e.memset(s_[:], 1.0), [], [b_ss[ss.index(s_)]])
                dma("pool", gbc[:], gffn[layer:layer + 1, :].partition_broadcast(128).squeeze(1), [], [b_gbc])
                dma("pool", cpt[:], cp_d[layer], [], [b_cp])

                tiles = []
                for si, seg in enumerate(SEGS):
                    a, b = seg["own0"] - halo, seg["own1"] + halo
                    n_tot = b - a
                    sizes = []
                    rem = n_tot
                    while rem > 2 * TILE_OUT:
                        sizes.append(TILE_OUT)
                        rem -= TILE_OUT
                    if rem > TILE_OUT:
                        first = min(TILE_OUT, ((rem + 1) // 2 + 127) // 128 * 128)
                        sizes += [first, rem - first]
                    else:
                        sizes.append(rem)
                    t = a
                    for n in sizes:
                        tiles.append((si, t, n))
                        t += n
                xcnt = {"n": 0}

                def norm_parts(ti):
                    si, t0, n = tiles[ti]
                    n_up = n + 2
                    k = ti % 2
                    nbu = (n_up + 127) // 128
                    parts = []
                    lx = {}

                    def L(bi):
                        if bi < nbu and bi not in lx:
                            m = min(128, n_up - bi * 128)
                            xi = xcnt["n"] % NXN
                            xcnt["n"] += 1
                            load_rows("pool", xn[xi][:m, :], src, t0 - 1 + bi * 128, m, [b_xn[xi]])
                            lx[bi] = xi
                    for bi in range(nbu):
                        m = min(128, n_up - bi * 128)

                        def A(bi=bi, m=m):
                            L(bi)
                            xi = lx[bi]
                            act(hb[0][:m, :], xn[xi][:m, :], AF.Square, [b_xn[xi]], [b_hb[0], b_ss[k]], accum=ss[k][:m, bi:bi + 1])
                            ts("dve", rt[k][:m, bi:bi + 1], ss[k][:m, bi:bi + 1], 1.0 / D, EPS, ALU.mult, ALU.add, [b_ss[k]], [b_rt[k]])
                            tt("pool", rstd[k][:m, bi:bi + 1], rt[k][:m, bi:bi + 1], mhalf[:m, 0:1], ALU.pow, [b_rt[k], b_mhalf], [b_rstd[k]])
                            stt("dve", hb[0][:m, :], xn[xi][:m, :], rstd[k][:m, bi:bi + 1], gbc[:m, :], ALU.mult, ALU.mult,
                                [b_xn[xi], b_rstd[k], b_gbc], [b_hb[0]])
                            L(bi + NXN)

                        def B(bi=bi, m=m):
                            transposes(hb[0], m, hT[k], bi * 128, [7], b_hb[0], b_hT[k])
                        parts.append((A, B))
                    parts_pre = lambda: [L(i) for i in range(NXN)]
                    return parts, parts_pre

                def norm_a(ti):
                    parts, pre = norm_parts(ti)
                    pre()
                    for A, B in parts:
                        A()
                        B()

                def up(ti, gi, hooks=None):
                    si, t0, n = tiles[ti]
                    n_up = n + 2
                    k = ti % 2
                    ab = (ti * NGRP + gi) % 2
                    conv_tick(1)
                    for jj in range(GP):
                        jp = gi * GP + jj
                        par = jp % 2
                        rg = wload(U_FFN[layer] + 2 * jp)
                        rv = wload(U_FFN[layer] + 2 * jp + 1)
                        for (rs_, bk) in ((rg, 2 * par), (rv, 2 * par + 1)):
                            def fu(e, rs_=rs_, bk=bk):
                                i = None
                                for d in range(NCH):
                                    i = e.matmul(bank(bk, n_up), lhsT=wring[:, rs_, d * 128:(d + 1) * 128], rhs=hT[k][:, d, 0:n_up],
                                                 start=(d == 0), stop=(d == NCH - 1))
                                return i
                            P.add("pe", fu, [ringbuf[rs_], b_hT[k]], [psb[bk]])
                        for (bk, ch, ybuf, b_y) in ((2 * par, jp, yg[par], b_yg[par]), (2 * par + 1, NFC + jp, yv[par], b_yv[par])):
                            c4 = ch * 4
                            act(ybuf[:, 0:n], bank(bk, n, 1), AF.Identity, [psb[bk], b_cp], [b_y],
                                bias=cpt[:, c4 + 3:c4 + 4], scale=cpt[:, c4 + 1:c4 + 2])
                            stt("dve", ybuf[:, 0:n], bank(bk, n, 0), cpt[:, c4:c4 + 1], ybuf[:, 0:n], ALU.mult, ALU.add,
                                [psb[bk], b_cp, b_y], [b_y])
                            stt("dve", ybuf[:, 0:n], bank(bk, n, 2), cpt[:, c4 + 2:c4 + 3], ybuf[:, 0:n], ALU.mult, ALU.add,
                                [psb[bk], b_cp, b_y], [b_y])
                        act(sg[par][:, 0:n], yg[par][:, 0:n], AF.Silu, [b_yg[par]], [b_sg[par]])
                        tt("dve", actT[ab][:, jj, 0:n], sg[par][:, 0:n], yv[par][:, 0:n], ALU.mult, [b_sg[par], b_yv[par]], [b_actT[ab]])
                        if hooks:
                            if 1 <= jj <= len(hooks):
                                hooks[jj - 1][1]()
                            if jj < len(hooks):
                                hooks[jj][0]()

                dcnt = {"n": 0}

                def init_oacc(ti):
                    si, t0, n = tiles[ti]
                    k = ti % 2
                    nbd = (n + 127) // 128
                    for b in range(nbd):
                        m = min(128, n - b * 128)
                        load_rows("pool", oacc[b][:m, :], src, t0 + b * 128, m, b_oacc[b])
                        dma("pool", vm[k][:m, b:b + 1], vmask[t0 + b * 128: t0 + b * 128 + m, :], [], [b_vm[k]])

                def down(ti, gi):
                    si, t0, n = tiles[ti]
                    k = ti % 2
                    ab = (ti * NGRP + gi) % 2
                    nbd = (n + 127) // 128
                    seg = SEGS[si]
                    rd = [wload(U_FFN[layer] + 88 + gi * GP + f) for f in range(GP)]
                    for b in range(nbd):
                        m = min(128, n - b * 128)
                        for dq in range(4):
                            bk = 4 + dcnt["n"] % 3
                            dcnt["n"] += 1

                            def fd(e, b=b, m=m, dq=dq, bk=bk):
                                i = None
                                for f in range(GP):
                                    i = e.matmul(ps[:m, bk * 512:(bk + 1) * 512], lhsT=actT[ab][:, f, b * 128:b * 128 + m],
                                                 rhs=wring[:, rd[f], dq * 512:(dq + 1) * 512], start=(f == 0), stop=(f == GP - 1))
                                return i
                            P.add("pe", fd, [b_actT[ab]] + [ringbuf[r] for r in rd], [psb[bk]])
                            tt("dve", oacc[b][:m, dq * 512:(dq + 1) * 512], oacc[b][:m, dq * 512:(dq + 1) * 512],
                               ps[:m, bk * 512:(bk + 1) * 512], ALU.add, [psb[bk], b_oacc[b][dq]], [b_oacc[b][dq]])
                        if gi == NGRP - 1:
                            r0 = t0 + b * 128
                            if not final:
                                act(oacc[b][:m, :], oacc[b][:m, :], AF.Copy, b_oacc[b] + [b_vm[k]], b_oacc[b], scale=vm[k][:m, b:b + 1])
                                store_rows("pool", dst, r0, m, oacc[b][:m, :], b_oacc[b])
                            else:
                                act(hb[0][:m, :], oacc[b][:m, :], AF.Square, b_oacc[b], [b_hb[0], b_ss2], accum=ss2[:m, b:b + 1])
                                ts("dve", rt2[:m, b:b + 1], ss2[:m, b:b + 1], 1.0 / D, EPS, ALU.mult, ALU.add, [b_ss2], [b_rt2])
                                tt("pool", rstd2[:m, b:b + 1], rt2[:m, b:b + 1], mhalf[:m, 0:1], ALU.pow, [b_rt2, b_mhalf], [b_rstd2])
                                stt("dve", oacc[b][:m, :], oacc[b][:m, :], rstd2[:m, b:b + 1], gfb[:m, :], ALU.mult, ALU.mult,
                                    b_oacc[b] + [b_rstd2, b_gfb], b_oacc[b])
                                yrow = (r0 - seg["own0"]) + (0 if si == 0 else SEGS[0]["nown"])
                                store_rows("pool", "y", r0, m, oacc[b][:m, :], b_oacc[b], yrow=yrow)

                nt = len(tiles)
                steps = [(ti, gi) for ti in range(nt) for gi in range(NGRP)]
                norm_a(0)
                init_oacc(0)
                nparts = None
                for s, (ti, gi) in enumerate(steps):
                    if gi == 1 and ti + 1 < nt:
                        nparts, npre = norm_parts(ti + 1)
                        npre()
                        up(ti, gi)
                    elif gi == 2 and ti + 1 < nt:
                        up(ti, gi, hooks=nparts)
                    else:
                        up(ti, gi)
                    if s >= 1:
                        pti, pgi = steps[s - 1]
                        down(pti, pgi)
                        if pgi == NGRP - 1 and pti + 1 < nt:
                            init_oacc(pti + 1)
                down(*steps[-1])
                P.barrier()
            return len(tiles)

        def attn_phase(layer, j, halo_blk, src, dst):
            slopes = [2.0 ** (-8.0 * (h + 1) / NHEAD) for h in range(NHEAD)]
            UA = U_ATTN[j]

            def heads_of_chunk(c):
                return (c, 8 + c) if c < 8 else (16 + c - 8, 24 + c - 8)
            with ExitStack() as st:
                def sb(name, shape, dt):
                    return st.enter_context(nc.sbuf_tensor(f"a{layer}_{name}", list(shape), dt))
                maxkv = max(s["nown"] // 128 + 2 * halo_blk + 2 for s in SEGS)
                KT = sb("KT", [128, 2, maxkv * 128], BF16)
                Vt = sb("Vt", [128, maxkv, 256], BF16)
                ST = 2
                hT = [sb(f"hT{i}", [128, NCH, ST * 128], BF16) for i in range(2)]
                QT = [sb(f"QT{i}", [128, NCH, ST * 128], BF16) for i in range(2)]
                NXN = 4
                xn = [sb(f"xn{i}", [128, D], F32) for i in range(NXN)]
                hb = [sb(f"hb{i}", [128, D], BF16) for i in range(1)]
                gbc = sb("gbc", [128, D], F32)
                z = [sb(f"z{i}", [128, 4, 385], F32) for i in range(2)]
                ee = [sb(f"e{i}", [128, 4, 385], BF16) for i in range(2)]
                pTs = [[sb(f"pTs{i}_{k}", [128, 2, 384], BF16) for k in range(2)] for i in range(2)]
                oT = [sb(f"oT{i}", [128, NCH, 128], BF16) for i in range(2)]
                base = sb("base", [128, 384], F32)
                baseblk = [sb(f"baseblk{i}", [128, 384], F32) for i in range(2)]
                kmb = [sb(f"kmb{i}", [128, 384], F32) for i in range(2)]
                sinkb = sb("sinkb", [128, NHEAD], F32)
                sink8 = sb("sink8", [128, NHEAD], F32)
                vm = [sb(f"vm{i}", [128, 1], F32) for i in range(4)]
                NS = 4
                ss = sb("ss", [128, NS], F32)
                rt = sb("rt", [128, NS], F32)
                rstd = sb("rstd", [128, NS], F32)
                mx = [sb(f"mx{i}", [128, 4], F32) for i in range(2)]
                negm = [sb(f"negm{i}", [128, 4], F32) for i in range(2)]
                rs_ = [sb(f"rs{i}", [128, 4], F32) for i in range(2)]
                rr = [sb(f"rr{i}", [128, 4], F32) for i in range(2)]
                b_KT, b_Vt, b_gbc, b_base, b_sinkb, b_sink8 = (Buf() for _ in range(6))
                b_hT = [Buf(), Buf()]
                b_QT = [Buf(), Buf()]
                b_xn = [Buf() for _ in range(NXN)]
                b_hb = [Buf()]
                b_z = [Buf(), Buf()]
                b_e = [Buf(), Buf()]
                b_pTs = [[Buf(), Buf()], [Buf(), Buf()]]
                b_oT = [Buf(), Buf()]
                b_bb = [Buf(), Buf()]
                b_kmb = [Buf(), Buf()]
                b_vm = [Buf() for _ in range(4)]
                b_ss = [Buf() for _ in range(NS)]
                b_rt = [Buf() for _ in range(NS)]
                b_rstd = [Buf() for _ in range(NS)]
                b_mx = [Buf(), Buf()]
                b_negm = [Buf(), Buf()]
                b_rs = [Buf(), Buf()]
                b_rr = [Buf(), Buf()]

                dma("pool", gbc[:], gmix[layer:layer + 1, :].partition_broadcast(128).squeeze(1), [], [b_gbc])
                dma("pool", base[:], base_d[:, :], [], [b_base])
                dma("pool", sinkb[:], sink_d[j:j + 1, :].partition_broadcast(128).squeeze(1), [], [b_sinkb])
                ts("dve", sink8[:], sinkb[:], 8.0, None, ALU.mult, None, [b_sinkb], [b_sink8])
                ncnt = {"x": 0, "s": 0, "mb": 0}
                psb_pv = [Buf() for _ in range(4)]

                def mbank():
                    ncnt["mb"] += 1
                    return 6 + ncnt["mb"] % 2

                def norm_L(blk):
                    xi = ncnt["x"] % NXN
                    ncnt["x"] += 1
                    load_rows("pool", xn[xi][:], src, blk * 128, 128, [b_xn[xi]])
                    return xi

                def norm_A(blk, xi=None):
                    if xi is None:
                        xi = norm_L(blk)
                    si_ = ncnt["s"] % NS
                    ncnt["s"] += 1
                    act(hb[0][:], xn[xi][:], AF.Square, [b_xn[xi]], [b_hb[0], b_ss[si_]], accum=ss[:, si_:si_ + 1])
                    ts("dve", rt[:, si_:si_ + 1], ss[:, si_:si_ + 1], 1.0 / D, EPS, ALU.mult, ALU.add, [b_ss[si_]], [b_rt[si_]])
                    tt("pool", rstd[:, si_:si_ + 1], rt[:, si_:si_ + 1], mhalf[:, 0:1], ALU.pow, [b_rt[si_], b_mhalf], [b_rstd[si_]])
                    stt("dve", hb[0][:], xn[xi][:], rstd[:, si_:si_ + 1], gbc[:], ALU.mult, ALU.mult,
                        [b_xn[xi], b_rstd[si_], b_gbc], [b_hb[0]])
                    return xi

                def norm_B(hti, col):
                    transposes(hb[0], 128, hT[hti], col * 128, [6, 7], b_hb[0], b_hT[hti])

                def norm_to_hT(blk, hti, col, xi=None):
                    xi = norm_A(blk, xi)
                    norm_B(hti, col)
                    return xi

                def proj_fm(uid, hti, dst_ap, ncols, b_dst, r=None):
                    if r is None:
                        r = wload(uid)
                    bk = mbank()

                    def f(e):
                        i = None
                        for d in range(NCH):
                            i = e.matmul(bank(bk, ncols), lhsT=wring[:, r, d * 128:(d + 1) * 128], rhs=hT[hti][:, d, 0:ncols],
                                         start=(d == 0), stop=(d == NCH - 1))
                        return i
                    P.add("pe", f, [ringbuf[r], b_hT[hti]], [psb[bk]])
                    act(dst_ap, bank(bk, ncols), AF.Copy, [psb[bk]], [b_dst])

                cnt = {"st": 0, "blk": 0, "wo": 0}

                for seg in SEGS:
                    qb0 = seg["own0"] // 128 - halo_blk
                    qb1 = seg["own1"] // 128 + halo_blk
                    kv0 = qb0 - 1
                    nkv = qb1 + 1 - kv0
                    pre = {c: norm_L(kv0 + c) for c in range(min(ST, nkv))}
                    a0_done = False
                    for s0 in range(0, nkv, ST):
                        nst = min(ST, nkv - s0)
                        hti = cnt["st"] % 2
                        cnt["st"] += 1
                        cur_pre = pre
                        pre = {c: norm_L(kv0 + s0 + ST + c) for c in range(min(ST, max(0, nkv - s0 - ST)))}
                        for c in range(nst):
                            if not (c == 0 and a0_done):
                                norm_A(kv0 + s0 + c, cur_pre[c])
                            norm_B(hti, c)
                        a0_done = False
                        if pre:
                            norm_A(kv0 + s0 + ST, pre[0])
                            a0_done = True
                        for kc in range(2):
                            proj_fm(UA + 16 + kc, hti, KT[:, kc, s0 * 128:(s0 + nst) * 128], nst * 128, b_KT)
                        rv = [wload(UA + 18 + dg) for dg in range(2)]
                        for c in range(nst):
                            bkv = mbank()

                            def fv(e, c=c, rv=rv, hti=hti, bkv=bkv):
                                i = None
                                for d in range(NCH):
                                    i = e.matmul(bank(bkv, 256), lhsT=hT[hti][:, d, c * 128:(c + 1) * 128],
                                                 rhs=wring[:, rv[d // 8], (d % 8) * 256:(d % 8 + 1) * 256],
                                                 start=(d == 0), stop=(d == NCH - 1))
                                return i
                            P.add("pe", fv, [b_hT[hti]] + [ringbuf[r] for r in rv], [psb[bkv]])
                            act(Vt[:, s0 + c, :], bank(bkv, 256), AF.Copy, [psb[bkv]], [b_Vt])
                    nq = qb1 - qb0
                    blkctx = {}

                    def prep_pieces(s0):
                        nst = min(ST, nq - s0)
                        hti = cnt["st"] % 2
                        cnt["st"] += 1
                        xis = {}
                        A, B, Q, L = [], [], [], []
                        lx = {}
                        for c in range(nst):
                            def pl(c=c):
                                lx[c] = norm_L(qb0 + s0 + c)
                            L.append(pl)

                            def pa(c=c):
                                xis[c] = norm_A(qb0 + s0 + c, lx.get(c))

                            def pb_(c=c):
                                norm_B(hti, c)
                            A.append(pa)
                            B.append(pb_)
                        qslots = {}

                        def pql():
                            for c in range(NCH):
                                qslots[c] = wload(UA + c)
                        for c in range(NCH):
                            def pq(c=c):
                                proj_fm(UA + c, hti, QT[hti][:, c, 0:nst * 128], nst * 128, b_QT[hti], r=qslots.get(c))
                            Q.append(pq)

                        def pc():
                            for c in range(nst):
                                qb = qb0 + s0 + c
                                bi = cnt["blk"] % 2
                                vi = cnt["blk"] % 4
                                cnt["blk"] += 1
                                kvi = qb - kv0
                                dma("pool", kmb[bi][:], kmask_d[0:1, (qb - 1) * 128:(qb + 2) * 128].partition_broadcast(128).squeeze(1), [], [b_kmb[bi]])
                                dma("pool", vm[vi][:], vmask[qb * 128:(qb + 1) * 128, :], [], [b_vm[vi]])
                                tt("dve", baseblk[bi][:], base[:], kmb[bi][:], ALU.min, [b_base, b_kmb[bi]], [b_bb[bi]])
                                blkctx[s0 + c] = dict(qb=qb, bi=bi, vi=vi, kvi=kvi, kc0=(kvi - 1) * 128, xi=xis[c], qti=hti, qc=c)
                        return dict(A=A, B=B, Q=Q, pc=pc, QL=pql, L=L)

                    groups = [(bq, hg) for bq in range(nq) for hg in range(8)]

                    def chunks_of(hg):
                        return (2 * hg, 2 * hg + 1)

                    def S0(gi):
                        bq, hg = groups[gi]
                        cx = blkctx[bq]
                        par = gi % 2
                        h4 = 0
                        for c in chunks_of(hg):
                            for side, h in enumerate(heads_of_chunk(c)):
                                pb = side * 64
                                kc = c // 8

                                def fs(e, c=c, pb=pb, kc=kc, h4=h4, cx=cx):
                                    return e.matmul(bank(h4, 384), lhsT=QT[cx["qti"]][pb:pb + 64, c, cx["qc"] * 128:(cx["qc"] + 1) * 128],
                                                    rhs=KT[pb:pb + 64, kc, cx["kc0"]:cx["kc0"] + 384], start=True, stop=True)
                                P.add("pe", fs, [b_QT[cx["qti"]], b_KT], [psb[h4]])
                                stt("dve", z[par][:, h4, 0:384], baseblk[cx["bi"]][:], 8.0 * slopes[h], bank(h4, 384), ALU.mult, ALU.add,
                                    [b_bb[cx["bi"]], psb[h4]], [b_z[par]])
                                cp_("dve", z[par][:, h4, 384:385], sink8[:, h:h + 1], [b_sink8], [b_z[par]])
                                h4 += 1
                        P.add("dve", lambda e, par=par: e.tensor_reduce(out=mx[par][:], in_=z[par][:], axis=AX.X, op=ALU.max),
                              [b_z[par]], [b_mx[par]])
                        ts("dve", negm[par][:], mx[par][:], -0.125, None, ALU.mult, None, [b_mx[par]], [b_negm[par]])

                    def S1(gi):
                        par = gi % 2
                        for h4 in range(4):
                            act(ee[par][:, h4, :], z[par][:, h4, :], AF.Exp, [b_z[par], b_negm[par]], [b_e[par], b_rs[par]],
                                bias=negm[par][:, h4:h4 + 1], scale=0.125, accum=rs_[par][:, h4:h4 + 1])

                    def S2(gi):
                        par = gi % 2
                        P.add("dve", lambda e, par=par: e.reciprocal(out=rr[par][:], in_=rs_[par][:]), [b_rs[par]], [b_rr[par]])
                        tt("pool", ee[par][:, :, 0:384], ee[par][:, :, 0:384], rr[par][:].unsqueeze(2).to_broadcast([128, 4, 384]), ALU.mult,
                           [b_e[par], b_rr[par]], [b_e[par]])

                    def S3(gi):
                        par = gi % 2
                        for hp in range(2):
                            pbk = 4 + hp
                            pTv = bank_bf(pbk)

                            def ftr(e, hp=hp, pTv=pTv, par=par):
                                i = None
                                for hh in range(2):
                                    for kbk in range(3):
                                        i = e.transpose(out=pTv[:, hh * 384 + kbk * 128: hh * 384 + (kbk + 1) * 128],
                                                        in_=ee[par][:, hp * 2 + hh, kbk * 128:(kbk + 1) * 128], identity=ident[:])
                                return i
                            P.add("pe", ftr, [b_e[par], b_ident], [psb[pbk]])
                            act(pTs[par][hp][:].rearrange("p a k -> p (a k)"), pTv[:, 0:768], AF.Copy, [psb[pbk]], [b_pTs[par][hp]])

                    def S4(gi):
                        bq, hg = groups[gi]
                        cx = blkctx[bq]
                        par = gi % 2
                        for hp, c in enumerate(chunks_of(hg)):
                            kc = c // 8
                            for side in range(2):
                                h4 = hp * 2 + side
                                pb = side * 64

                                def fpv(e, hp=hp, kc=kc, cx=cx, par=par, side=side, h4=h4):
                                    i = None
                                    for kbk in range(3):
                                        i = e.matmul(bank(h4, 128, 384), lhsT=Vt[:, cx["kvi"] - 1 + kbk, kc * 128:(kc + 1) * 128],
                                                     rhs=pTs[par][hp][:, side, kbk * 128:(kbk + 1) * 128], start=(kbk == 0), stop=(kbk == 2))
                                    return i
                                P.add("pe", fpv, [b_Vt, b_pTs[par][hp]], [psb[h4]])
                                col = h4 * 512 + 384
                                act(oT[cx["bi"]][pb:pb + 64, c, :], ps[pb:pb + 64, col:col + 128], AF.Copy, [psb[h4]], [b_oT[cx["bi"]]])

                    ro_pre = {}

                    def wo_pieces(bq):
                        cx = blkctx[bq]
                        bi, xi, vi = cx["bi"], cx["xi"], cx["vi"]
                        state = {}
                        pieces = []
                        for dq in range(4):
                            def pw(dq=dq):
                                if dq == 0:
                                    state["ro"] = ro_pre.pop(bq) if bq in ro_pre else [wload(UA + 20 + jc) for jc in range(NCH)]
                                ro = state["ro"]
                                bk = mbank()

                                def fo(e):
                                    i = None
                                    for jc in range(NCH):
                                        i = e.matmul(bank(bk), lhsT=oT[bi][:, jc, :], rhs=wring[:, ro[jc], dq * 512:(dq + 1) * 512],
                                                     start=(jc == 0), stop=(jc == NCH - 1))
                                    return i
                                P.add("pe", fo, [b_oT[bi]] + [ringbuf[r] for r in ro], [psb[bk]])
                                stt("dve", xn[xi][:, dq * 512:(dq + 1) * 512], bank(bk), vm[vi][:, 0:1], xn[xi][:, dq * 512:(dq + 1) * 512],
                                    ALU.mult, ALU.add, [psb[bk], b_vm[vi], b_xn[xi]], [b_xn[xi]])
                                if dq == 3:
                                    store_rows("pool", dst, cx["qb"] * 128, 128, xn[xi][:], [b_xn[xi]])
                            pieces.append(pw)
                        return pieces

                    ng = len(groups)
                    GST = ST * 8
                    pp = prep_pieces(0)
                    for i_ in range(len(pp["A"])):
                        pp["A"][i_]()
                        pp["B"][i_]()
                    for p_ in pp["Q"]:
                        p_()
                    pcfn = pp["pc"]
                    queue = []
                    nxp = None
                    wo_prev = []
                    for it in range(ng + 4):
                        if it - 4 >= 0:
                            S4(it - 4)
                            if groups[it - 4][1] == 7:
                                wp = wo_pieces(groups[it - 4][0])
                                if (it - 4) % GST == GST - 1 and nxp is not None:
                                    wo_prev = wp
                                else:
                                    queue.extend(wp)
                        if 0 <= it - 3 < ng:
                            S3(it - 3)
                        if 0 <= it - 2 < ng:
                            S2(it - 2)
                        if 0 <= it - 1 < ng:
                            S1(it - 1)
                        if it < ng:
                            bq, hg = groups[it]
                            k = it % GST
                            if k == 0:
                                while queue:
                                    queue.pop(0)()
                                pcfn()
                                if bq >= 1:
                                    ro_pre[bq - 1] = [wload(UA + 20 + jc) for jc in range(NCH)]
                                nxt = bq + ST
                                nxp = prep_pieces(nxt) if nxt < nq else None
                                pcfn = nxp["pc"] if nxp else None
                                if nxp is not None:
                                    nxp["L"][0]()
                            if k == 2 and nxp is not None:
                                nxp["A"][0]()
                            if k == 4 and nxp is not None:
                                q_ = list(wo_prev)
                                wo_prev = []
                                if len(nxp["L"]) > 1:
                                    q_.append(nxp["L"][1])
                                q_.append(nxp["QL"])
                                q_.append(nxp["B"][0])
                                if len(nxp["A"]) > 1:
                                    q_.append(nxp["A"][1])
                                    q_.append(nxp["B"][1])
                                q_.extend(nxp["Q"])
                                queue.extend(q_)
                            S0(it)
                            if it % 8 == 5:
                                conv_tick(1)
                        for _ in range(2):
                            if queue:
                                queue.pop(0)()
                    for p_ in wo_prev:
                        queue.append(p_)
                    while queue:
                        queue.pop(0)()
                P.barrier()

        cur = "x_in"
        need = (U_POOL[0] + 4, U_FFN[0] + 132, U_ATTN[0] + 40, U_FFN[1] + 132, U_POOL[1] + 4, U_FFN[2] + 132, U_ATTN[1] + 40, U_FFN[3] + 132)
        for pi, (kind, layer, idx, halo) in enumerate(PHASES[:nphases]):
            dstn = "xa" if cur in ("x_in", "xb") else "xb"
            last = (pi == len(PHASES) - 1)
            emit_conversions(need[pi])
            if pi == 1:
                zero_tick(NB)
            if kind == "pool":
                pool_phase(layer, idx, halo, cur, dstn)
            elif kind == "attn":
                attn_phase(layer, idx, halo, cur, dstn)
            else:
                ffn_phase(layer, halo, cur, "y" if last else dstn, last)
            cur = dstn
        fin = list(out_ops)
        if debug:
            srcn = cur if cur != "y" else "xa"
            for b in range(NB):
                fin.append(dma("sp", dbg[b * 128:(b + 1) * 128, :], dap[srcn][b * 128:(b + 1) * 128, :], [dbuf[srcn][b]], []))
        P.emit(nc, gst, {"sp": fin})
    return nc


def _tile_cols(w, c0, ncol=128):
    blk = w[:, c0:c0 + ncol]
    return blk.reshape(NCH, 128, ncol).transpose(1, 0, 2).reshape(128, NCH * ncol)


def _build_wsrc(inp):
    ws = np.empty((NU, 128, 2048), np.float32)
    for j in range(2):
        u = U_POOL[j]
        for g in range(4):
            ws[u + g] = inp["pool_w"][j, g].reshape(4, 128, 512).transpose(1, 0, 2).reshape(128, 2048)
        u = U_ATTN[j]
        wq = inp["attn_wqkv"][j]
        wo = inp["attn_wo"][j]
        for c in range(16):
            ha, hb_ = (c, 8 + c) if c < 8 else (16 + c - 8, 24 + c - 8)
            cols = np.concatenate([np.arange(ha * 64, ha * 64 + 64), np.arange(hb_ * 64, hb_ * 64 + 64)])
            qc = wq[:, cols]
            ws[u + c] = qc.reshape(NCH, 128, 128).transpose(1, 0, 2).reshape(128, 2048)
            ws[u + 20 + c] = wo[cols, :]
        for kc in range(2):
            ws[u + 16 + kc] = _tile_cols(wq, 2048 + kc * 128)
        wv = wq[:, 2304:2560]
        for dg in range(2):
            ws[u + 18 + dg] = wv[dg * 1024:(dg + 1) * 1024].reshape(8, 128, 256).transpose(1, 0, 2).reshape(128, 2048)
        ws[u + 36:u + 40] = 0.0
    for i in range(4):
        u = U_FFN[i]
        wup = inp["ffn_wup"][i]
        wdn = inp["ffn_wdown"][i]
        for jp in range(NFC):
            ws[u + 2 * jp] = _tile_cols(wup, jp * 128)
            ws[u + 2 * jp + 1] = _tile_cols(wup, DFF + jp * 128)
        for f in range(NFC):
            ws[u + 88 + f] = wdn[f * 128:(f + 1) * 128, :]
    return ws


def _consts():
    q = np.arange(128)[:, None]
    k = np.arange(384)[None, :]
    rel = np.abs(k - 128 - q)
    base = np.where(rel <= 128, -rel.astype(np.float32), np.float32(-1.0e6)).astype(np.float32)
    wband = np.zeros((128, 4, 3, 128), np.float32)
    s = np.arange(128)[:, None]
    t = np.arange(128)[None, :]
    for g, win in enumerate(POOL_WINDOWS):
        for r in range(3):
            sp = s + (r - 1) * 128
            wband[:, g, r, :] = ((sp >= t - win // 2) & (sp < t + win // 2)).astype(np.float32)
    ident = np.eye(128, dtype=np.float32)
    return base, wband.reshape(128, 12 * 128).astype(NPBF), ident.astype(NPBF)


def _core_stream(c, x_prompt, x_sample):
    xs = np.zeros((NTOK, D), np.float32)
    vm = np.zeros((NTOK, 1), np.float32)
    a0 = 2048 * c - HB * 128
    lo, hi = max(a0, 0), min(a0 + SEGS[0]["nblk"] * 128, x_sample.shape[1])
    xs[SEGS[0]["s0"] + lo - a0: SEGS[0]["s0"] + hi - a0] = x_sample[0, lo:hi]
    vm[SEGS[0]["s0"] + lo - a0: SEGS[0]["s0"] + hi - a0] = 1.0
    j, half = c // 2, c % 2
    b0 = 1024 * half - HB * 128
    lo, hi = max(b0, 0), min(b0 + SEGS[1]["nblk"] * 128, x_prompt.shape[1])
    xs[SEGS[1]["s0"] + lo - b0: SEGS[1]["s0"] + hi - b0] = x_prompt[j, lo:hi]
    vm[SEGS[1]["s0"] + lo - b0: SEGS[1]["s0"] + hi - b0] = 1.0
    kb = np.where(vm[:, 0] > 0, 0.0, -1.0e6).astype(np.float32).reshape(1, NTOK)
    return xs, vm, kb


def make_in_maps(inp):
    f32 = lambda a: np.ascontiguousarray(np.asarray(a, dtype=np.float32))
    inp = {k: f32(v) for k, v in inp.items()}
    ws = _build_wsrc(inp)
    base, wband, ident = _consts()
    cp = np.empty((4, 128, 88, 4), np.float32)
    for i in range(4):
        cw = inp["ffn_conv_w"][i]
        cb = inp["ffn_conv_b"][i]
        for kk in range(3):
            cp[i, :, :, kk] = cw[kk].reshape(88, 128).T
        cp[i, :, :, 3] = cb.reshape(88, 128).T
    cp = cp.reshape(4, 128, 88 * 4)
    common = {"wsrc": ws, "gmix": inp["norm_mix"], "gffn": inp["norm_ffn"], "gfin": inp["norm_final"].reshape(1, D),
              "pscale": inp["pool_scale"], "sink": inp["attn_sink"], "cp": cp, "base": base, "wband": wband, "ident": ident}
    maps = []
    for c in range(NCORES):
        xs, vm, kb = _core_stream(c, inp["x_prompt"], inp["x_sample"])
        m = dict(common)
        m.update({"x_in": xs, "vmask": vm, "kmask": kb})
        maps.append(m)
    return maps


_NC_CACHE = {}


def kernel(**inputs):
    if "nc" not in _NC_CACHE:
        _NC_CACHE["nc"] = build_program()
    nc = _NC_CACHE["nc"]
    maps = make_in_maps(inputs)
    res = run_bass_kernel_spmd(nc, maps, core_ids=list(range(NCORES)))
    y_prompt = np.empty((4, 2048, D), np.float32)
    y_sample = np.empty((1, 16384, D), np.float32)
    for c in range(NCORES):
        y = np.asarray(res.results[c]["y"])
        y_sample[0, 2048 * c:2048 * (c + 1)] = y[:2048]
        y_prompt[c // 2, 1024 * (c % 2):1024 * (c % 2 + 1)] = y[2048:]
    return (y_prompt, y_sample)
```

```python
import numpy as np
import ml_dtypes
from contextlib import ExitStack
import concourse.bass as bass
import concourse.mybir as mybir
from concourse.bass_utils import run_bass_kernel_spmd

F32 = mybir.dt.float32
BF16 = mybir.dt.bfloat16
AF = mybir.ActivationFunctionType
ALU = mybir.AluOpType
AX = mybir.AxisListType
NPBF = ml_dtypes.bfloat16

D = 2048
DFF = 5632
NCH = 16
NFC = 44
NGRP = 4
GP = 11
NHEAD = 32
EPS = 1e-6
NCORES = 8
HB = 3
SEGS = (dict(s0=128, nblk=22, own0=512, own1=2560, nown=2048),
        dict(s0=2944, nblk=14, own0=3328, own1=4352, nown=1024))
NB = 38
NTOK = NB * 128
NOUT = 3072
R_SLOTS = 16
TILE_OUT = 510
POOL_WINDOWS = (2, 4, 8, 16)

U_POOL = {}
U_ATTN = {}
U_FFN = {}
_u = 0
for _i in range(4):
    if _i % 2 == 0:
        U_POOL[_i // 2] = _u
        _u += 4
    else:
        U_ATTN[_i // 2] = _u
        _u += 40
    U_FFN[_i] = _u
    _u += 132
NU = _u
CONV_CHUNK = 4

PHASES = (("pool", 0, 0, 3), ("ffn", 0, 0, 267), ("attn", 1, 0, 2), ("ffn", 1, 1, 138),
          ("pool", 2, 1, 2), ("ffn", 2, 2, 129), ("attn", 3, 1, 1), ("ffn", 3, 3, 0))

ENGS = ("pe", "act", "dve", "pool", "sp")
EPOCH = 16000
NDMASEM = 24


class Buf:
    __slots__ = ("name", "w", "r")

    def __init__(self, name=""):
        self.name = name
        self.w = None
        self.r = []


class Op:
    __slots__ = ("eng", "fn", "deps", "sig", "tok", "dma", "k")

    def __init__(self, eng, fn, dma):
        self.eng = eng
        self.fn = fn
        self.dma = dma
        self.deps = []
        self.sig = dma
        self.tok = None
        self.k = None


class Prog:
    def __init__(self):
        self.ops = {e: [] for e in ENGS}
        self.ndma = {e: 0 for e in ENGS}

    def add(self, eng, fn, reads=(), writes=(), dma=False):
        op = Op(eng, fn, dma)
        raw = set()
        deps = set()
        for b in reads:
            if b.w is not None:
                raw.add(b.w)
                deps.add(b.w)
        for b in writes:
            if b.w is not None:
                deps.add(b.w)
            deps.update(b.r)
        for d in deps:
            if d is op:
                continue
            if (not d.dma) and (not dma) and d.eng == eng:
                if eng == "pe" or d not in raw:
                    continue
            d.sig = True
            op.deps.append(d)
        for b in reads:
            b.r.append(op)
        for b in writes:
            b.w = op
            b.r = []
        if dma:
            op.k = self.ndma[eng]
            self.ndma[eng] += 1
        self.ops[eng].append(op)
        return op

    def barrier(self, engs=("pe", "act", "dve", "pool")):
        last = {}
        for e in engs:
            lst = self.ops[e]
            j = len(lst) - 1
            while j >= 0 and lst[j].fn is None:
                j -= 1
            last[e] = lst[j] if j >= 0 else None
        dmas = [o for o in self.ops["pool"] if o.dma][-NDMASEM:]
        for e in engs:
            m = Op(e, None, False)
            for e2 in engs:
                if e2 != e and last[e2] is not None:
                    last[e2].sig = True
                    m.deps.append(last[e2])
            m.deps.extend(dmas)
            self.ops[e].append(m)

    def emit(self, nc, stack, final_waits):
        ctr = {}
        dsem = {}
        for e in ENGS:
            nsig = sum(1 for o in self.ops[e] if o.sig and not o.dma)
            ctr[e] = [stack.enter_context(nc.semaphore(f"c_{e}_{i}")) for i in range((nsig + EPOCH - 1) // EPOCH)]
            if self.ndma[e]:
                dsem[e] = [stack.enter_context(nc.semaphore(f"d_{e}_{i}")) for i in range(NDMASEM)]
        for e in ENGS:
            i = 0
            for o in self.ops[e]:
                if o.dma:
                    o.tok = (dsem[e][o.k % NDMASEM], 16 * (o.k // NDMASEM + 1))
                elif o.sig:
                    o.tok = (ctr[e][i // EPOCH], i % EPOCH + 1)
                    i += 1
        block = stack.enter_context(nc.Block())
        prog = self

        def run(eng_name, h):
            waited = {}

            def w(tok):
                s, v = tok
                key = id(s)
                if waited.get(key, 0) < v:
                    h.wait_ge(s, v)
                    waited[key] = v

            for o in prog.ops[eng_name]:
                for d in o.deps:
                    w(d.tok)
                if o.dma and o.k >= NDMASEM:
                    w((o.tok[0], o.tok[1] - 16))
                if o.fn is None:
                    continue
                inst = o.fn(h)
                if o.sig:
                    inst.then_inc(o.tok[0], 16 if o.dma else 1)
            for o in final_waits.get(eng_name, ()):
                w(o.tok)

        @block.tensor
        def _(h):
            run("pe", h)

        @block.scalar
        def _(h):
            run("act", h)

        @block.vector
        def _(h):
            run("dve", h)

        @block.gpsimd
        def _(h):
            run("pool", h)

        @block.sync
        def _(h):
            run("sp", h)


NU_AT = (4, 136, 176, 308, 312, 444, 484, 616)


def build_program(nphases=8, debug=False):
    nc = bass.Bass("TRN2", target_bir_lowering=False)
    P = Prog()
    NUE = NU_AT[nphases - 1]

    def din(name, shape, dt=F32):
        return nc.dram_tensor(name, list(shape), dt, kind="ExternalInput").ap()

    x_in = din("x_in", [NTOK, D])
    vmask = din("vmask", [NTOK, 1])
    kmask_d = din("kmask", [1, NTOK])
    wsrc = din("wsrc", [NUE, 128, 2048])
    gmix = din("gmix", [4, D])
    gffn = din("gffn", [4, D])
    gfin = din("gfin", [1, D])
    pscale = din("pscale", [2, D])
    sink_d = din("sink", [2, NHEAD])
    cp_d = din("cp", [4, 128, 88 * 4])
    base_d = din("base", [128, 384])
    wband_d = din("wband", [128, 12 * 128], BF16)
    ident_d = din("ident", [128, 128], BF16)
    y_out = nc.dram_tensor("y", [NOUT, D], F32, kind="ExternalOutput").ap()
    WPART = 156
    wbf_parts = [nc.dram_tensor(f"wbf{i}", [WPART, 128, 2048], BF16, kind="Internal").ap()
                 for i in range((NU + WPART - 1) // WPART)]
    xa = nc.dram_tensor("xa", [NTOK, D], F32, kind="Internal").ap()
    xb = nc.dram_tensor("xb", [NTOK, D], F32, kind="Internal").ap()
    dbg = nc.dram_tensor("dbg", [NTOK, D], F32, kind="ExternalOutput").ap() if debug else None

    with ExitStack() as gst:
        def gsb(name, shape, dt):
            return gst.enter_context(nc.sbuf_tensor(name, list(shape), dt))

        wring = gsb("wring", [128, R_SLOTS, 2048], BF16)
        ident = gsb("ident_sb", [128, 128], BF16)
        mhalf = gsb("mhalf", [128, 8], F32)
        zero_t = gsb("zero_t", [128, 2048], BF16)
        ps = gst.enter_context(nc.psum_tensor("ps", [128, 4096], F32))

        ringbuf = [Buf(f"ring{i}") for i in range(R_SLOTS)]
        psb = [Buf(f"psb{i}") for i in range(8)]
        convbuf = [Buf(f"conv{i}") for i in range((NU + CONV_CHUNK - 1) // CONV_CHUNK)]
        b_ident = Buf("ident")
        b_mhalf = Buf("mhalf")
        b_zero = Buf("zero")
        dbuf = {"x_in": [Buf() for _ in range(NB)], "xa": [Buf() for _ in range(NB)],
                "xb": [Buf() for _ in range(NB)], "y": [Buf() for _ in range(NOUT // 128)]}
        dap = {"x_in": x_in, "xa": xa, "xb": xb, "y": y_out}

        def bank(i, n=512, off=0):
            return ps[:, i * 512 + off: i * 512 + off + n]

        def bank_bf(i):
            return ps[:, i * 512:(i + 1) * 512].bitcast(BF16)

        def blocks_of(r0, n):
            return list(range(r0 // 128, (r0 + n - 1) // 128 + 1))

        def dma(q, out, in_, reads, writes):
            return P.add(q, lambda e: e.dma_start(out=out, in_=in_), reads, writes, dma=True)

        def load_rows(q, dst_ap, name, r0, m, writes):
            return dma(q, dst_ap, dap[name][r0:r0 + m, :], [dbuf[name][b] for b in blocks_of(r0, m)], writes)

        out_ops = []

        def store_rows(q, name, r0, m, src_ap, reads, yrow=None):
            if name == "y":
                o = dma(q, y_out[yrow:yrow + m, :], src_ap, reads, [dbuf["y"][b] for b in blocks_of(yrow, m)])
                out_ops.append(o)
                return o
            return dma(q, dap[name][r0:r0 + m, :], src_ap, reads, [dbuf[name][b] for b in blocks_of(r0, m)])

        def act(out, in_, func, reads, writes, bias=None, scale=None, accum=None):
            kw = {}
            if bias is not None:
                kw["bias"] = bias
            if scale is not None:
                kw["scale"] = scale
            if accum is not None:
                kw["accum_out"] = accum
            return P.add("act", lambda e: e.activation(out=out, in_=in_, func=func, **kw), reads, writes)

        def stt(eng, out, in0, scalar, in1, op0, op1, reads, writes):
            return P.add(eng, lambda e: e.scalar_tensor_tensor(out=out, in0=in0, scalar=scalar, in1=in1, op0=op0, op1=op1),
                         reads, writes)

        def ts(eng, out, in0, s1, s2, op0, op1, reads, writes):
            if s2 is None:
                return P.add(eng, lambda e: e.tensor_scalar(out=out, in0=in0, scalar1=s1, scalar2=None, op0=op0), reads, writes)
            return P.add(eng, lambda e: e.tensor_scalar(out=out, in0=in0, scalar1=s1, scalar2=s2, op0=op0, op1=op1), reads, writes)

        def tt(eng, out, in0, in1, op, reads, writes):
            return P.add(eng, lambda e: e.tensor_tensor(out=out, in0=in0, in1=in1, op=op), reads, writes)

        def cp_(eng, out, in_, reads, writes):
            return P.add(eng, lambda e: e.tensor_copy(out=out, in_=in_), reads, writes)

        wstate = {"step": 0}

        def wload(uid):
            r = wstate["step"] % R_SLOTS
            wstate["step"] += 1
            dma("sp", wring[:, r, :], wbf_parts[uid // WPART][uid % WPART], [convbuf[uid // CONV_CHUNK]], [ringbuf[r]])
            return r

        conv_state = {"next": 0}

        def emit_conversions(upto_uid):
            upto = min((min(upto_uid, NUE) + CONV_CHUNK - 1) // CONV_CHUNK, len(convbuf))
            while conv_state["next"] < upto:
                k = conv_state["next"]
                lo, hi = k * CONV_CHUNK, min((k + 1) * CONV_CHUNK, NUE)
                dma("pool", wbf_parts[lo // WPART][lo % WPART: lo % WPART + (hi - lo)], wsrc[lo:hi], [], [convbuf[k]])
                conv_state["next"] += 1

        def conv_tick(n=1):
            for _ in range(n):
                if conv_state["next"] < len(convbuf) and conv_state["next"] * CONV_CHUNK < NUE:
                    emit_conversions((conv_state["next"] + 1) * CONV_CHUNK)

        zstate = {"next": 0}

        def zero_tick(n=1):
            for _ in range(n):
                b = zstate["next"]
                if b < NB:
                    dma("pool", xb[b * 128:(b + 1) * 128, :], zero_t[:], [b_zero], [dbuf["xb"][b]])
                    zstate["next"] += 1

        dma("pool", ident[:], ident_d[:, :], [], [b_ident])
        P.add("pool", lambda e: e.memset(mhalf[:], -0.5), [], [b_mhalf])
        P.add("pool", lambda e: e.memset(zero_t[:], 0.0), [], [b_zero])
        emit_conversions(U_POOL[0] + 4)

        def norm_stats(xn_ap, m, ss_ap, junk_ap, b_xn, b_junk, b_ss):
            act(junk_ap[:m, :], xn_ap[:m, :], AF.Square, [b_xn], [b_junk, b_ss], accum=ss_ap[:m, :])

        def norm_rstd(ss_ap, rt_ap, rstd_ap, ncol, b_ss, b_rt, b_rstd):
            ts("dve", rt_ap[:, :ncol], ss_ap[:, :ncol], 1.0 / D, EPS, ALU.mult, ALU.add, [b_ss], [b_rt])
            tt("pool", rstd_ap[:, :ncol], rt_ap[:, :ncol], mhalf[:, :ncol], ALU.pow, [b_rt, b_mhalf], [b_rstd])

        def transposes(hb_ap, m, hT_ap3, col0, banks, b_hb, b_hT):
            for half in range(2):
                bk = banks[half % len(banks)]
                pT = bank_bf(bk)

                def f(e, half=half, pT=pT):
                    i = None
                    for c in range(8):
                        cc = half * 8 + c
                        i = e.transpose(out=pT[:, c * 128:c * 128 + m], in_=hb_ap[:m, cc * 128:(cc + 1) * 128],
                                        identity=ident[:m, :m])
                    return i
                P.add("pe", f, [b_hb, b_ident], [psb[bk]])
                src = pT.rearrange("p (c t) -> p c t", c=8)[:, :, :m]
                act(hT_ap3[:, half * 8:(half + 1) * 8, col0:col0 + m], src, AF.Copy, [psb[bk]], [b_hT])

        def pool_phase(layer, j, halo_blk, src, dst):
            with ExitStack() as st:
                def sb(name, shape, dt):
                    return st.enter_context(nc.sbuf_tensor(f"p{layer}_{name}", list(shape), dt))
                NX = 5
                xn = [sb(f"xn{i}", [128, D], F32) for i in range(NX)]
                hb = [sb(f"hb{i}", [128, D], BF16) for i in range(NX)]
                vb = [sb(f"vb{i}", [128, 128], BF16) for i in range(NX)]
                vm = [sb(f"vm{i}", [128, 1], F32) for i in range(NX)]
                gbc = sb("gbc", [128, D], F32)
                psc = sb("psc", [128, D], F32)
                wband = sb("wband", [128, 12, 128], BF16)
                ones128 = sb("ones128", [128, 128], BF16)
                Mt = [sb(f"M{i}", [128, 12, 128], BF16) for i in range(2)]
                rcnt = sb("rcnt", [128, 4, 128], F32)
                pT = [sb(f"pT{i}", [128, 16, 128], BF16) for i in range(2)]
                t1 = [sb(f"t1{i}", [128, D], F32) for i in range(2)]
                xo = [sb(f"xo{i}", [128, D], F32) for i in range(2)]
                ss = sb("ss", [128, NX], F32)
                rt = sb("rt", [128, NX], F32)
                rstd = sb("rstd", [128, NX], F32)
                b_xn = [Buf() for _ in range(NX)]
                b_hb = [Buf() for _ in range(NX)]
                b_vb = [Buf() for _ in range(NX)]
                b_vm = [Buf() for _ in range(NX)]
                b_ss = [Buf() for _ in range(NX)]
                b_rt = [Buf() for _ in range(NX)]
                b_rstd = [Buf() for _ in range(NX)]
                b_gbc, b_psc, b_wband, b_ones, b_rcnt = Buf(), Buf(), Buf(), Buf(), Buf()
                b_M = [Buf(), Buf()]
                b_pT = [Buf(), Buf()]
                b_t1 = [Buf(), Buf()]
                b_xo = [Buf(), Buf()]

                dma("pool", gbc[:], gmix[layer:layer + 1, :].partition_broadcast(128).squeeze(1), [], [b_gbc])
                dma("pool", psc[:], pscale[j:j + 1, :].partition_broadcast(128).squeeze(1), [], [b_psc])
                dma("pool", wband[:].rearrange("p a t -> p (a t)"), wband_d[:, :], [], [b_wband])
                P.add("pool", lambda e: e.memset(ones128[:], 1.0), [], [b_ones])

                def load_block(bb):
                    i = bb % NX
                    load_rows("pool", xn[i][:], src, bb * 128, 128, [b_xn[i]])
                    dma("pool", vm[i][:], vmask[bb * 128:(bb + 1) * 128, :], [], [b_vm[i]])

                def norm_block(bb):
                    i = bb % NX
                    norm_stats(xn[i], 128, ss[:, i:i + 1], hb[i], b_xn[i], b_hb[i], b_ss[i])
                    ts("dve", rt[:, i:i + 1], ss[:, i:i + 1], 1.0 / D, EPS, ALU.mult, ALU.add, [b_ss[i]], [b_rt[i]])
                    tt("pool", rstd[:, i:i + 1], rt[:, i:i + 1], mhalf[:, 0:1], ALU.pow, [b_rt[i], b_mhalf], [b_rstd[i]])
                    stt("dve", hb[i][:], xn[i][:], rstd[:, i:i + 1], gbc[:], ALU.mult, ALU.mult,
                        [b_xn[i], b_rstd[i], b_gbc], [b_hb[i]])
                    ts("dve", vb[i][:], ones128[:], vm[i][:, 0:1], None, ALU.mult, None, [b_ones, b_vm[i]], [b_vb[i]])

                cnt = {"n": 0}

                def pool_stage1a(b):
                    k = b % 2
                    nb3 = [(b - 1) % NX, b % NX, (b + 1) % NX]

                    def fc(e):
                        i = None
                        for g in range(4):
                            for r in range(3):
                                i = e.matmul(bank(7, 128, g * 128), lhsT=vb[nb3[r]][:], rhs=wband[:, g * 3 + r, :],
                                             start=(r == 0), stop=(r == 2))
                        return i
                    P.add("pe", fc, [b_vb[x] for x in nb3] + [b_wband], [psb[7]])
                    ts("dve", rcnt[:].rearrange("p g t -> p (g t)"), bank(7), 1.0, None, ALU.max, None, [psb[7]], [b_rcnt])
                    P.add("dve", lambda e: e.reciprocal(out=rcnt[:], in_=rcnt[:]), [b_rcnt], [b_rcnt])
                    M4 = Mt[k][:].rearrange("p (g r) t -> p g r t", g=4)
                    W4 = wband[:].rearrange("p (g r) t -> p g r t", g=4)
                    tt("dve", M4, W4, rcnt[:].unsqueeze(2).to_broadcast([128, 4, 3, 128]), ALU.mult,
                       [b_wband, b_rcnt], [b_M[k]])
                    tt("dve", M4[:, :, 1, :], M4[:, :, 1, :], ident[:].unsqueeze(1).to_broadcast([128, 4, 128]),
                       ALU.subtract, [b_M[k], b_ident], [b_M[k]])

                def pool_stage1b(b):
                    k = b % 2
                    nb3 = [(b - 1) % NX, b % NX, (b + 1) % NX]
                    for g in range(4):
                        def fp(e, g=g):
                            i = None
                            for cc in range(4):
                                for r in range(3):
                                    i = e.matmul(bank(g, 128, cc * 128),
                                                 lhsT=hb[nb3[r]][:, g * 512 + cc * 128: g * 512 + (cc + 1) * 128],
                                                 rhs=Mt[k][:, g * 3 + r, :], start=(r == 0), stop=(r == 2))
                            return i
                        P.add("pe", fp, [b_hb[x] for x in nb3] + [b_M[k]], [psb[g]])
                        act(pT[k][:, g * 4:(g + 1) * 4, :].rearrange("p c t -> p (c t)"), bank(g), AF.Copy, [psb[g]], [b_pT[k]])

                def pool_stage2(b):
                    k = b % 2
                    for g in range(4):
                        rs = wload(U_POOL[j] + g)

                        yb = 4 + cnt["n"] % 3
                        cnt["n"] += 1

                        def fy(e, g=g, rs=rs, yb=yb):
                            i = None
                            for cc in range(4):
                                i = e.matmul(bank(yb), lhsT=pT[k][:, g * 4 + cc, :], rhs=wring[:, rs, cc * 512:(cc + 1) * 512],
                                             start=(cc == 0), stop=(cc == 3))
                            return i
                        P.add("pe", fy, [b_pT[k], ringbuf[rs]], [psb[yb]])
                        stt("dve", t1[k][:, g * 512:(g + 1) * 512], bank(yb), vm[b % NX][:, 0:1], psc[:, g * 512:(g + 1) * 512],
                            ALU.mult, ALU.mult, [psb[yb], b_vm[b % NX], b_psc], [b_t1[k]])
                    tt("dve", xo[k][:], t1[k][:], xn[b % NX][:], ALU.add, [b_t1[k], b_xn[b % NX]], [b_xo[k]])
                    store_rows("pool", dst, b * 128, 128, xo[k][:], [b_xo[k]])

                for seg in SEGS:
                    b0 = seg["own0"] // 128 - halo_blk
                    b1 = seg["own1"] // 128 + halo_blk
                    load_block(b0 - 1)
                    for bb in range(b0 - 1, b1 + 4):
                        if b0 <= bb - 4 < b1:
                            pool_stage2(bb - 4)
                        if bb + 1 <= b1:
                            load_block(bb + 1)
                        if b0 <= bb - 3 < b1:
                            pool_stage1b(bb - 3)
                        if b0 <= bb - 2 < b1:
                            pool_stage1a(bb - 2)
                        if bb <= b1:
                            norm_block(bb)
                        conv_tick(1)
                        zero_tick(1)
                P.barrier()

        def ffn_phase(layer, halo, src, dst, final):
            with ExitStack() as st:
                def sb(name, shape, dt):
                    return st.enter_context(nc.sbuf_tensor(f"f{layer}_{name}", list(shape), dt))
                actT = [sb(f"actT{i}", [128, GP, 512], BF16) for i in range(2)]
                hT = [sb(f"hT{i}", [128, NCH, 512], BF16) for i in range(2)]
                NXN = 2
                xn = [sb(f"xn{i}", [128, D], F32) for i in range(NXN)]
                hb = [sb(f"hb{i}", [128, D], BF16) for i in range(1)]
                oacc = [sb(f"oacc{i}", [128, D], F32) for i in range(4)]
                yg = [sb(f"yg{i}", [128, 512], F32) for i in range(2)]
                yv = [sb(f"yv{i}", [128, 512], F32) for i in range(2)]
                sg = [sb(f"sg{i}", [128, 512], F32) for i in range(2)]
                gbc = sb("gbc", [128, D], F32)
                cpt = sb("cp", [128, 88 * 4], F32)
                ss = [sb(f"ss{i}", [128, 4], F32) for i in range(2)]
                rt = [sb(f"rt{i}", [128, 4], F32) for i in range(2)]
                rstd = [sb(f"rstd{i}", [128, 4], F32) for i in range(2)]
                vm = [sb(f"vm{i}", [128, 4], F32) for i in range(2)]
                b_actT = [Buf(), Buf()]
                b_hT = [Buf(), Buf()]
                b_xn = [Buf() for _ in range(NXN)]
                b_hb = [Buf()]
                b_oacc = [[Buf() for _ in range(4)] for _ in range(4)]
                b_yg = [Buf(), Buf()]
                b_yv = [Buf(), Buf()]
                b_sg = [Buf(), Buf()]
                b_gbc, b_cp = Buf(), Buf()
                b_ss = [Buf(), Buf()]
                b_rt = [Buf(), Buf()]
                b_rstd = [Buf(), Buf()]
                b_vm = [Buf(), Buf()]
                if final:
                    gfb = sb("gfb", [128, D], F32)
                    ss2 = sb("ss2", [128, 4], F32)
                    rt2 = sb("rt2", [128, 4], F32)
                    rstd2 = sb("rstd2", [128, 4], F32)
                    b_gfb, b_ss2, b_rt2, b_rstd2 = Buf(), Buf(), Buf(), Buf()
                    dma("pool", gfb[:], gfin[0:1, :].partition_broadcast(128).squeeze(1), [], [b_gfb])
                for s_ in ss:
                    P.add("pool", lambda e, s_=s_: e.memset(s_[:], 1.0), [], [b_ss[ss.index(s_)]])
                dma("pool", gbc[:], gffn[layer:layer + 1, :].partition_broadcast(128).squeeze(1), [], [b_gbc])
                dma("pool", cpt[:], cp_d[layer], [], [b_cp])

                tiles = []
                for si, seg in enumerate(SEGS):
                    a, b = seg["own0"] - halo, seg["own1"] + halo
                    n_tot = b - a
                    sizes = []
                    rem = n_tot
                    while rem > 2 * TILE_OUT:
                        sizes.append(TILE_OUT)
                        rem -= TILE_OUT
                    if rem > TILE_OUT:
                        second = max(128, (rem // 2) // 128 * 128)
                        first = rem - second
                        if first > TILE_OUT:
                            first = min(TILE_OUT, ((rem + 1) // 2 + 127) // 128 * 128)
                            second = rem - first
                        sizes += [first, second]
                    else:
                        sizes.append(rem)
                    t = a
                    for n in sizes:
                        tiles.append((si, t, n))
                        t += n
                xcnt = {"n": 0}

                def norm_parts(ti):
                    si, t0, n = tiles[ti]
                    n_up = n + 2
                    k = ti % 2
                    nbu = (n_up + 127) // 128
                    parts = []
                    lx = {}

                    def L(bi):
                        if bi < nbu and bi not in lx:
                            m = min(128, n_up - bi * 128)
                            xi = xcnt["n"] % NXN
                            xcnt["n"] += 1
                            load_rows("pool", xn[xi][:m, :], src, t0 - 1 + bi * 128, m, [b_xn[xi]])
                            lx[bi] = xi
                    for bi in range(nbu):
                        m = min(128, n_up - bi * 128)

                        def A(bi=bi, m=m):
                            L(bi)
                            xi = lx[bi]
                            act(hb[0][:m, :], xn[xi][:m, :], AF.Square, [b_xn[xi]], [b_hb[0], b_ss[k]], accum=ss[k][:m, bi:bi + 1])
                            ts("dve", rt[k][:m, bi:bi + 1], ss[k][:m, bi:bi + 1], 1.0 / D, EPS, ALU.mult, ALU.add, [b_ss[k]], [b_rt[k]])
                            tt("pool", rstd[k][:m, bi:bi + 1], rt[k][:m, bi:bi + 1], mhalf[:m, 0:1], ALU.pow, [b_rt[k], b_mhalf], [b_rstd[k]])
                            stt("dve", hb[0][:m, :], xn[xi][:m, :], rstd[k][:m, bi:bi + 1], gbc[:m, :], ALU.mult, ALU.mult,
                                [b_xn[xi], b_rstd[k], b_gbc], [b_hb[0]])
                            L(bi + NXN)

                        def B(bi=bi, m=m):
                            transposes(hb[0], m, hT[k], bi * 128, [7], b_hb[0], b_hT[k])
                        parts.append((A, B))
                    parts_pre = lambda: [L(i) for i in range(NXN)]
                    return parts, parts_pre

                def norm_a(ti):
                    parts, pre = norm_parts(ti)
                    pre()
                    for A, B in parts:
                        A()
                        B()

                def up(ti, gi, hooks=None):
                    si, t0, n = tiles[ti]
                    n_up = n + 2
                    k = ti % 2
                    ab = (ti * NGRP + gi) % 2
                    conv_tick(1)
                    for jj in range(GP):
                        jp = gi * GP + jj
                        par = jp % 2
                        rg = wload(U_FFN[layer] + 2 * jp)
                        rv = wload(U_FFN[layer] + 2 * jp + 1)
                        for (rs_, bk) in ((rg, 2 * par), (rv, 2 * par + 1)):
                            def fu(e, rs_=rs_, bk=bk):
                                i = None
                                for d in range(NCH):
                                    i = e.matmul(bank(bk, n_up), lhsT=wring[:, rs_, d * 128:(d + 1) * 128], rhs=hT[k][:, d, 0:n_up],
                                                 start=(d == 0), stop=(d == NCH - 1))
                                return i
                            P.add("pe", fu, [ringbuf[rs_], b_hT[k]], [psb[bk]])
                        for (bk, ch, ybuf, b_y) in ((2 * par, jp, yg[par], b_yg[par]), (2 * par + 1, NFC + jp, yv[par], b_yv[par])):
                            c4 = ch * 4
                            act(ybuf[:, 0:n], bank(bk, n, 1), AF.Identity, [psb[bk], b_cp], [b_y],
                                bias=cpt[:, c4 + 3:c4 + 4], scale=cpt[:, c4 + 1:c4 + 2])
                            stt("dve", ybuf[:, 0:n], bank(bk, n, 0), cpt[:, c4:c4 + 1], ybuf[:, 0:n], ALU.mult, ALU.add,
                                [psb[bk], b_cp, b_y], [b_y])
                            stt("dve", ybuf[:, 0:n], bank(bk, n, 2), cpt[:, c4 + 2:c4 + 3], ybuf[:, 0:n], ALU.mult, ALU.add,
                                [psb[bk], b_cp, b_y], [b_y])
                        act(sg[par][:, 0:n], yg[par][:, 0:n], AF.Silu, [b_yg[par]], [b_sg[par]])
                        tt("dve", actT[ab][:, jj, 0:n], sg[par][:, 0:n], yv[par][:, 0:n], ALU.mult, [b_sg[par], b_yv[par]], [b_actT[ab]])
                        if hooks:
                            if 1 <= jj <= len(hooks):
                                hooks[jj - 1][1]()
                            if jj < len(hooks):
                                hooks[jj][0]()

                dcnt = {"n": 0}

                def init_oacc(ti):
                    si, t0, n = tiles[ti]
                    k = ti % 2
                    nbd = (n + 127) // 128
                    for b in range(nbd):
                        m = min(128, n - b * 128)
                        load_rows("pool", oacc[b][:m, :], src, t0 + b * 128, m, b_oacc[b])
                        dma("pool", vm[k][:m, b:b + 1], vmask[t0 + b * 128: t0 + b * 128 + m, :], [], [b_vm[k]])

                def down(ti, gi):
                    si, t0, n = tiles[ti]
                    k = ti % 2
                    ab = (ti * NGRP + gi) % 2
                    nbd = (n + 127) // 128
                    seg = SEGS[si]
                    rd = [wload(U_FFN[layer] + 88 + gi * GP + f) for f in range(GP)]
                    for b in range(nbd):
                        m = min(128, n - b * 128)
                        for dq in range(4):
                            bk = 4 + dcnt["n"] % 3
                            dcnt["n"] += 1

                            def fd(e, b=b, m=m, dq=dq, bk=bk):
                                i = None
                                for f in range(GP):
                                    i = e.matmul(ps[:m, bk * 512:(bk + 1) * 512], lhsT=actT[ab][:, f, b * 128:b * 128 + m],
                                                 rhs=wring[:, rd[f], dq * 512:(dq + 1) * 512], start=(f == 0), stop=(f == GP - 1))
                                return i
                            P.add("pe", fd, [b_actT[ab]] + [ringbuf[r] for r in rd], [psb[bk]])
                            tt("dve", oacc[b][:m, dq * 512:(dq + 1) * 512], oacc[b][:m, dq * 512:(dq + 1) * 512],
                               ps[:m, bk * 512:(bk + 1) * 512], ALU.add, [psb[bk], b_oacc[b][dq]], [b_oacc[b][dq]])
                        if gi == NGRP - 1:
                            r0 = t0 + b * 128
                            if not final:
                                act(oacc[b][:m, :], oacc[b][:m, :], AF.Copy, b_oacc[b] + [b_vm[k]], b_oacc[b], scale=vm[k][:m, b:b + 1])
                                store_rows("pool", dst, r0, m, oacc[b][:m, :], b_oacc[b])
                            else:
                                act(hb[0][:m, :], oacc[b][:m, :], AF.Square, b_oacc[b], [b_hb[0], b_ss2], accum=ss2[:m, b:b + 1])
                                ts("dve", rt2[:m, b:b + 1], ss2[:m, b:b + 1], 1.0 / D, EPS, ALU.mult, ALU.add, [b_ss2], [b_rt2])
                                tt("pool", rstd2[:m, b:b + 1], rt2[:m, b:b + 1], mhalf[:m, 0:1], ALU.pow, [b_rt2, b_mhalf], [b_rstd2])
                                stt("dve", oacc[b][:m, :], oacc[b][:m, :], rstd2[:m, b:b + 1], gfb[:m, :], ALU.mult, ALU.mult,
                                    b_oacc[b] + [b_rstd2, b_gfb], b_oacc[b])
                                yrow = (r0 - seg["own0"]) + (0 if si == 0 else SEGS[0]["nown"])
                                store_rows("pool", "y", r0, m, oacc[b][:m, :], b_oacc[b], yrow=yrow)

                nt = len(tiles)
                steps = [(ti, gi) for ti in range(nt) for gi in range(NGRP)]
                norm_a(0)
                init_oacc(0)
                nparts = None
                for s, (ti, gi) in enumerate(steps):
                    if gi == 1 and ti + 1 < nt:
                        nparts, npre = norm_parts(ti + 1)
                        npre()
                        up(ti, gi)
                    elif gi == 2 and ti + 1 < nt:
                        up(ti, gi, hooks=nparts)
                    else:
                        up(ti, gi)
                    if s >= 1:
                        pti, pgi = steps[s - 1]
                        down(pti, pgi)
                        if pgi == NGRP - 1 and pti + 1 < nt:
                            init_oacc(pti + 1)
                down(*steps[-1])
                P.barrier()
            return len(tiles)

        def attn_phase(layer, j, halo_blk, src, dst):
            slopes = [2.0 ** (-8.0 * (h + 1) / NHEAD) for h in range(NHEAD)]
            UA = U_ATTN[j]

            def heads_of_chunk(c):
                return (c, 8 + c) if c < 8 else (16 + c - 8, 24 + c - 8)
            with ExitStack() as st:
                def sb(name, shape, dt):
                    return st.enter_context(nc.sbuf_tensor(f"a{layer}_{name}", list(shape), dt))
                maxkv = max(s["nown"] // 128 + 2 * halo_blk + 2 for s in SEGS)
                KT = sb("KT", [128, 2, maxkv * 128], BF16)
                Vt = sb("Vt", [128, maxkv, 256], BF16)
                ST = 2
                hT = [sb(f"hT{i}", [128, NCH, ST * 128], BF16) for i in range(2)]
                QT = [sb(f"QT{i}", [128, NCH, ST * 128], BF16) for i in range(2)]
                NXN = 4
                xn = [sb(f"xn{i}", [128, D], F32) for i in range(NXN)]
                hb = [sb(f"hb{i}", [128, D], BF16) for i in range(1)]
                gbc = sb("gbc", [128, D], F32)
                z = [sb(f"z{i}", [128, 4, 385], F32) for i in range(2)]
                ee = [sb(f"e{i}", [128, 4, 385], BF16) for i in range(2)]
                pTs = [[sb(f"pTs{i}_{k}", [128, 2, 384], BF16) for k in range(2)] for i in range(2)]
                oT = [sb(f"oT{i}", [128, NCH, 128], BF16) for i in range(2)]
                base = sb("base", [128, 384], F32)
                baseblk = [sb(f"baseblk{i}", [128, 384], F32) for i in range(2)]
                kmb = [sb(f"kmb{i}", [128, 384], F32) for i in range(2)]
                sinkb = sb("sinkb", [128, NHEAD], F32)
                sink8 = sb("sink8", [128, NHEAD], F32)
                vm = [sb(f"vm{i}", [128, 1], F32) for i in range(4)]
                NS = 4
                ss = sb("ss", [128, NS], F32)
                rt = sb("rt", [128, NS], F32)
                rstd = sb("rstd", [128, NS], F32)
                mx = [sb(f"mx{i}", [128, 4], F32) for i in range(2)]
                negm = [sb(f"negm{i}", [128, 4], F32) for i in range(2)]
                rs_ = [sb(f"rs{i}", [128, 4], F32) for i in range(2)]
                rr = [sb(f"rr{i}", [128, 4], F32) for i in range(2)]
                b_KT, b_Vt, b_gbc, b_base, b_sinkb, b_sink8 = (Buf() for _ in range(6))
                b_hT = [Buf(), Buf()]
                b_QT = [Buf(), Buf()]
                b_xn = [Buf() for _ in range(NXN)]
                b_hb = [Buf()]
                b_z = [Buf(), Buf()]
                b_e = [Buf(), Buf()]
                b_pTs = [[Buf(), Buf()], [Buf(), Buf()]]
                b_oT = [Buf(), Buf()]
                b_bb = [Buf(), Buf()]
                b_kmb = [Buf(), Buf()]
                b_vm = [Buf() for _ in range(4)]
                b_ss = [Buf() for _ in range(NS)]
                b_rt = [Buf() for _ in range(NS)]
                b_rstd = [Buf() for _ in range(NS)]
                b_mx = [Buf(), Buf()]
                b_negm = [Buf(), Buf()]
                b_rs = [Buf(), Buf()]
                b_rr = [Buf(), Buf()]

                dma("pool", gbc[:], gmix[layer:layer + 1, :].partition_broadcast(128).squeeze(1), [], [b_gbc])
                dma("pool", base[:], base_d[:, :], [], [b_base])
                dma("pool", sinkb[:], sink_d[j:j + 1, :].partition_broadcast(128).squeeze(1), [], [b_sinkb])
                ts("dve", sink8[:], sinkb[:], 8.0, None, ALU.mult, None, [b_sinkb], [b_sink8])
                ncnt = {"x": 0, "s": 0, "mb": 0}
                psb_pv = [Buf() for _ in range(4)]

                def mbank():
                    ncnt["mb"] += 1
                    return 6 + ncnt["mb"] % 2

                def norm_L(blk):
                    xi = ncnt["x"] % NXN
                    ncnt["x"] += 1
                    load_rows("pool", xn[xi][:], src, blk * 128, 128, [b_xn[xi]])
                    return xi

                def norm_A(blk, xi=None):
                    if xi is None:
                        xi = norm_L(blk)
                    si_ = ncnt["s"] % NS
                    ncnt["s"] += 1
                    act(hb[0][:], xn[xi][:], AF.Square, [b_xn[xi]], [b_hb[0], b_ss[si_]], accum=ss[:, si_:si_ + 1])
                    ts("dve", rt[:, si_:si_ + 1], ss[:, si_:si_ + 1], 1.0 / D, EPS, ALU.mult, ALU.add, [b_ss[si_]], [b_rt[si_]])
                    tt("pool", rstd[:, si_:si_ + 1], rt[:, si_:si_ + 1], mhalf[:, 0:1], ALU.pow, [b_rt[si_], b_mhalf], [b_rstd[si_]])
                    stt("dve", hb[0][:], xn[xi][:], rstd[:, si_:si_ + 1], gbc[:], ALU.mult, ALU.mult,
                        [b_xn[xi], b_rstd[si_], b_gbc], [b_hb[0]])
                    return xi

                def norm_B(hti, col):
                    transposes(hb[0], 128, hT[hti], col * 128, [6, 7], b_hb[0], b_hT[hti])

                def norm_to_hT(blk, hti, col, xi=None):
                    xi = norm_A(blk, xi)
                    norm_B(hti, col)
                    return xi

                def proj_fm(uid, hti, dst_ap, ncols, b_dst, r=None):
                    if r is None:
                        r = wload(uid)
                    bk = mbank()

                    def f(e):
                        i = None
                        for d in range(NCH):
                            i = e.matmul(bank(bk, ncols), lhsT=wring[:, r, d * 128:(d + 1) * 128], rhs=hT[hti][:, d, 0:ncols],
                                         start=(d == 0), stop=(d == NCH - 1))
                        return i
                    P.add("pe", f, [ringbuf[r], b_hT[hti]], [psb[bk]])
                    act(dst_ap, bank(bk, ncols), AF.Copy, [psb[bk]], [b_dst])

                cnt = {"st": 0, "blk": 0, "wo": 0}

                for seg in SEGS:
                    qb0 = seg["own0"] // 128 - halo_blk
                    qb1 = seg["own1"] // 128 + halo_blk
                    kv0 = qb0 - 1
                    nkv = qb1 + 1 - kv0
                    pre = {c: norm_L(kv0 + c) for c in range(min(ST, nkv))}
                    for s0 in range(0, nkv, ST):
                        nst = min(ST, nkv - s0)
                        hti = cnt["st"] % 2
                        cnt["st"] += 1
                        cur_pre = pre
                        pre = {c: norm_L(kv0 + s0 + ST + c) for c in range(min(ST, max(0, nkv - s0 - ST)))}
                        for c in range(nst):
                            norm_to_hT(kv0 + s0 + c, hti, c, xi=cur_pre[c])
                        for kc in range(2):
                            proj_fm(UA + 16 + kc, hti, KT[:, kc, s0 * 128:(s0 + nst) * 128], nst * 128, b_KT)
                        rv = [wload(UA + 18 + dg) for dg in range(2)]
                        for c in range(nst):
                            bkv = mbank()

                            def fv(e, c=c, rv=rv, hti=hti, bkv=bkv):
                                i = None
                                for d in range(NCH):
                                    i = e.matmul(bank(bkv, 256), lhsT=hT[hti][:, d, c * 128:(c + 1) * 128],
                                                 rhs=wring[:, rv[d // 8], (d % 8) * 256:(d % 8 + 1) * 256],
                                                 start=(d == 0), stop=(d == NCH - 1))
                                return i
                            P.add("pe", fv, [b_hT[hti]] + [ringbuf[r] for r in rv], [psb[bkv]])
                            act(Vt[:, s0 + c, :], bank(bkv, 256), AF.Copy, [psb[bkv]], [b_Vt])
                    nq = qb1 - qb0
                    blkctx = {}

                    def prep_pieces(s0):
                        nst = min(ST, nq - s0)
                        hti = cnt["st"] % 2
                        cnt["st"] += 1
                        xis = {}
                        A, B, Q, L = [], [], [], []
                        lx = {}
                        for c in range(nst):
                            def pl(c=c):
                                lx[c] = norm_L(qb0 + s0 + c)
                            L.append(pl)

                            def pa(c=c):
                                xis[c] = norm_A(qb0 + s0 + c, lx.get(c))

                            def pb_(c=c):
                                norm_B(hti, c)
                            A.append(pa)
                            B.append(pb_)
                        qslots = {}

                        def pql():
                            for c in range(NCH):
                                qslots[c] = wload(UA + c)
                        for c in range(NCH):
                            def pq(c=c):
                                proj_fm(UA + c, hti, QT[hti][:, c, 0:nst * 128], nst * 128, b_QT[hti], r=qslots.get(c))
                            Q.append(pq)

                        def pc():
                            for c in range(nst):
                                qb = qb0 + s0 + c
                                bi = cnt["blk"] % 2
                                vi = cnt["blk"] % 4
                                cnt["blk"] += 1
                                kvi = qb - kv0
                                dma("pool", kmb[bi][:], kmask_d[0:1, (qb - 1) * 128:(qb + 2) * 128].partition_broadcast(128).squeeze(1), [], [b_kmb[bi]])
                                dma("pool", vm[vi][:], vmask[qb * 128:(qb + 1) * 128, :], [], [b_vm[vi]])
                                tt("dve", baseblk[bi][:], base[:], kmb[bi][:], ALU.min, [b_base, b_kmb[bi]], [b_bb[bi]])
                                blkctx[s0 + c] = dict(qb=qb, bi=bi, vi=vi, kvi=kvi, kc0=(kvi - 1) * 128, xi=xis[c], qti=hti, qc=c)
                        return dict(A=A, B=B, Q=Q, pc=pc, QL=pql, L=L)

                    groups = [(bq, hg) for bq in range(nq) for hg in range(8)]

                    def chunks_of(hg):
                        return (2 * hg, 2 * hg + 1)

                    def S0(gi):
                        bq, hg = groups[gi]
                        cx = blkctx[bq]
                        par = gi % 2
                        h4 = 0
                        for c in chunks_of(hg):
                            for side, h in enumerate(heads_of_chunk(c)):
                                pb = side * 64
                                kc = c // 8

                                def fs(e, c=c, pb=pb, kc=kc, h4=h4, cx=cx):
                                    return e.matmul(bank(h4, 384), lhsT=QT[cx["qti"]][pb:pb + 64, c, cx["qc"] * 128:(cx["qc"] + 1) * 128],
                                                    rhs=KT[pb:pb + 64, kc, cx["kc0"]:cx["kc0"] + 384], start=True, stop=True)
                                P.add("pe", fs, [b_QT[cx["qti"]], b_KT], [psb[h4]])
                                stt("dve", z[par][:, h4, 0:384], baseblk[cx["bi"]][:], 8.0 * slopes[h], bank(h4, 384), ALU.mult, ALU.add,
                                    [b_bb[cx["bi"]], psb[h4]], [b_z[par]])
                                cp_("dve", z[par][:, h4, 384:385], sink8[:, h:h + 1], [b_sink8], [b_z[par]])
                                h4 += 1
                        P.add("dve", lambda e, par=par: e.tensor_reduce(out=mx[par][:], in_=z[par][:], axis=AX.X, op=ALU.max),
                              [b_z[par]], [b_mx[par]])
                        ts("dve", negm[par][:], mx[par][:], -0.125, None, ALU.mult, None, [b_mx[par]], [b_negm[par]])

                    def S1(gi):
                        par = gi % 2
                        for h4 in range(4):
                            act(ee[par][:, h4, :], z[par][:, h4, :], AF.Exp, [b_z[par], b_negm[par]], [b_e[par], b_rs[par]],
                                bias=negm[par][:, h4:h4 + 1], scale=0.125, accum=rs_[par][:, h4:h4 + 1])

                    def S2(gi):
                        par = gi % 2
                        P.add("dve", lambda e, par=par: e.reciprocal(out=rr[par][:], in_=rs_[par][:]), [b_rs[par]], [b_rr[par]])
                        tt("pool", ee[par][:, :, 0:384], ee[par][:, :, 0:384], rr[par][:].unsqueeze(2).to_broadcast([128, 4, 384]), ALU.mult,
                           [b_e[par], b_rr[par]], [b_e[par]])

                    def S3(gi):
                        par = gi % 2
                        for hp in range(2):
                            pbk = 4 + hp
                            pTv = bank_bf(pbk)

                            def ftr(e, hp=hp, pTv=pTv, par=par):
                                i = None
                                for hh in range(2):
                                    for kbk in range(3):
                                        i = e.transpose(out=pTv[:, hh * 384 + kbk * 128: hh * 384 + (kbk + 1) * 128],
                                                        in_=ee[par][:, hp * 2 + hh, kbk * 128:(kbk + 1) * 128], identity=ident[:])
                                return i
                            P.add("pe", ftr, [b_e[par], b_ident], [psb[pbk]])
                            act(pTs[par][hp][:].rearrange("p a k -> p (a k)"), pTv[:, 0:768], AF.Copy, [psb[pbk]], [b_pTs[par][hp]])

                    def S4(gi):
                        bq, hg = groups[gi]
                        cx = blkctx[bq]
                        par = gi % 2
                        for hp, c in enumerate(chunks_of(hg)):
                            kc = c // 8
                            for side in range(2):
                                h4 = hp * 2 + side
                                pb = side * 64

                                def fpv(e, hp=hp, kc=kc, cx=cx, par=par, side=side, h4=h4):
                                    i = None
                                    for kbk in range(3):
                                        i = e.matmul(bank(h4, 128, 384), lhsT=Vt[:, cx["kvi"] - 1 + kbk, kc * 128:(kc + 1) * 128],
                                                     rhs=pTs[par][hp][:, side, kbk * 128:(kbk + 1) * 128], start=(kbk == 0), stop=(kbk == 2))
                                    return i
                                P.add("pe", fpv, [b_Vt, b_pTs[par][hp]], [psb[h4]])
                                col = h4 * 512 + 384
                                act(oT[cx["bi"]][pb:pb + 64, c, :], ps[pb:pb + 64, col:col + 128], AF.Copy, [psb[h4]], [b_oT[cx["bi"]]])

                    ro_pre = {}

                    def wo_pieces(bq):
                        cx = blkctx[bq]
                        bi, xi, vi = cx["bi"], cx["xi"], cx["vi"]
                        state = {}
                        pieces = []
                        for dq in range(4):
                            def pw(dq=dq):
                                if dq == 0:
                                    state["ro"] = ro_pre.pop(bq) if bq in ro_pre else [wload(UA + 20 + jc) for jc in range(NCH)]
                                ro = state["ro"]
                                bk = mbank()

                                def fo(e):
                                    i = None
                                    for jc in range(NCH):
                                        i = e.matmul(bank(bk), lhsT=oT[bi][:, jc, :], rhs=wring[:, ro[jc], dq * 512:(dq + 1) * 512],
                                                     start=(jc == 0), stop=(jc == NCH - 1))
                                    return i
                                P.add("pe", fo, [b_oT[bi]] + [ringbuf[r] for r in ro], [psb[bk]])
                                stt("dve", xn[xi][:, dq * 512:(dq + 1) * 512], bank(bk), vm[vi][:, 0:1], xn[xi][:, dq * 512:(dq + 1) * 512],
                                    ALU.mult, ALU.add, [psb[bk], b_vm[vi], b_xn[xi]], [b_xn[xi]])
                                if dq == 3:
                                    store_rows("pool", dst, cx["qb"] * 128, 128, xn[xi][:], [b_xn[xi]])
                            pieces.append(pw)
                        return pieces

                    ng = len(groups)
                    GST = ST * 8
                    pp = prep_pieces(0)
                    for i_ in range(len(pp["A"])):
                        pp["A"][i_]()
                        pp["B"][i_]()
                    for p_ in pp["Q"]:
                        p_()
                    pcfn = pp["pc"]
                    queue = []
                    nxp = None
                    wo_prev = []
                    for it in range(ng + 4):
                        if it - 4 >= 0:
                            S4(it - 4)
                            if groups[it - 4][1] == 7:
                                wp = wo_pieces(groups[it - 4][0])
                                if (it - 4) % GST == GST - 1 and nxp is not None:
                                    wo_prev = wp
                                else:
                                    queue.extend(wp)
                        if 0 <= it - 3 < ng:
                            S3(it - 3)
                        if 0 <= it - 2 < ng:
                            S2(it - 2)
                        if 0 <= it - 1 < ng:
                            S1(it - 1)
                        if it < ng:
                            bq, hg = groups[it]
                            k = it % GST
                            if k == 0:
                                while queue:
                                    queue.pop(0)()
                                pcfn()
                                if bq >= 1:
                                    ro_pre[bq - 1] = [wload(UA + 20 + jc) for jc in range(NCH)]
                                nxt = bq + ST
                                nxp = prep_pieces(nxt) if nxt < nq else None
                                pcfn = nxp["pc"] if nxp else None
                                if nxp is not None:
                                    nxp["L"][0]()
                            if k == 2 and nxp is not None:
                                nxp["A"][0]()
                            if k == 4 and nxp is not None:
                                q_ = list(wo_prev)
                                wo_prev = []
                                if len(nxp["L"]) > 1:
                                    q_.append(nxp["L"][1])
                                q_.append(nxp["QL"])
                                q_.append(nxp["B"][0])
                                if len(nxp["A"]) > 1:
                                    q_.append(nxp["A"][1])
                                    q_.append(nxp["B"][1])
                                q_.extend(nxp["Q"])
                                queue.extend(q_)
                            S0(it)
                            if it % 8 == 5:
                                conv_tick(1)
                        for _ in range(2):
                            if queue:
                                queue.pop(0)()
                    for p_ in wo_prev:
                        queue.append(p_)
                    while queue:
                        queue.pop(0)()
                P.barrier()

        cur = "x_in"
        need = (U_POOL[0] + 4, U_FFN[0] + 132, U_ATTN[0] + 40, U_FFN[1] + 132, U_POOL[1] + 4, U_FFN[2] + 132, U_ATTN[1] + 40, U_FFN[3] + 132)
        for pi, (kind, layer, idx, halo) in enumerate(PHASES[:nphases]):
            dstn = "xa" if cur in ("x_in", "xb") else "xb"
            last = (pi == len(PHASES) - 1)
            emit_conversions(need[pi])
            if pi == 1:
                zero_tick(NB)
            if kind == "pool":
                pool_phase(layer, idx, halo, cur, dstn)
            elif kind == "attn":
                attn_phase(layer, idx, halo, cur, dstn)
            else:
                ffn_phase(layer, halo, cur, "y" if last else dstn, last)
            cur = dstn
        fin = list(out_ops)
        if debug:
            srcn = cur if cur != "y" else "xa"
            for b in range(NB):
                fin.append(dma("sp", dbg[b * 128:(b + 1) * 128, :], dap[srcn][b * 128:(b + 1) * 128, :], [dbuf[srcn][b]], []))
        P.emit(nc, gst, {"sp": fin})
    return nc


def _tile_cols(w, c0, ncol=128):
    blk = w[:, c0:c0 + ncol]
    return blk.reshape(NCH, 128, ncol).transpose(1, 0, 2).reshape(128, NCH * ncol)


def _build_wsrc(inp):
    ws = np.empty((NU, 128, 2048), np.float32)
    for j in range(2):
        u = U_POOL[j]
        for g in range(4):
            ws[u + g] = inp["pool_w"][j, g].reshape(4, 128, 512).transpose(1, 0, 2).reshape(128, 2048)
        u = U_ATTN[j]
        wq = inp["attn_wqkv"][j]
        wo = inp["attn_wo"][j]
        for c in range(16):
            ha, hb_ = (c, 8 + c) if c < 8 else (16 + c - 8, 24 + c - 8)
            cols = np.concatenate([np.arange(ha * 64, ha * 64 + 64), np.arange(hb_ * 64, hb_ * 64 + 64)])
            qc = wq[:, cols]
            ws[u + c] = qc.reshape(NCH, 128, 128).transpose(1, 0, 2).reshape(128, 2048)
            ws[u + 20 + c] = wo[cols, :]
        for kc in range(2):
            ws[u + 16 + kc] = _tile_cols(wq, 2048 + kc * 128)
        wv = wq[:, 2304:2560]
        for dg in range(2):
            ws[u + 18 + dg] = wv[dg * 1024:(dg + 1) * 1024].reshape(8, 128, 256).transpose(1, 0, 2).reshape(128, 2048)
        ws[u + 36:u + 40] = 0.0
    for i in range(4):
        u = U_FFN[i]
        wup = inp["ffn_wup"][i]
        wdn = inp["ffn_wdown"][i]
        for jp in range(NFC):
            ws[u + 2 * jp] = _tile_cols(wup, jp * 128)
            ws[u + 2 * jp + 1] = _tile_cols(wup, DFF + jp * 128)
        for f in range(NFC):
            ws[u + 88 + f] = wdn[f * 128:(f + 1) * 128, :]
    return ws


def _consts():
    q = np.arange(128)[:, None]
    k = np.arange(384)[None, :]
    rel = np.abs(k - 128 - q)
    base = np.where(rel <= 128, -rel.astype(np.float32), np.float32(-1.0e6)).astype(np.float32)
    wband = np.zeros((128, 4, 3, 128), np.float32)
    s = np.arange(128)[:, None]
    t = np.arange(128)[None, :]
    for g, win in enumerate(POOL_WINDOWS):
        for r in range(3):
            sp = s + (r - 1) * 128
            wband[:, g, r, :] = ((sp >= t - win // 2) & (sp < t + win // 2)).astype(np.float32)
    ident = np.eye(128, dtype=np.float32)
    return base, wband.reshape(128, 12 * 128).astype(NPBF), ident.astype(NPBF)


def _core_stream(c, x_prompt, x_sample):
    xs = np.zeros((NTOK, D), np.float32)
    vm = np.zeros((NTOK, 1), np.float32)
    a0 = 2048 * c - HB * 128
    lo, hi = max(a0, 0), min(a0 + SEGS[0]["nblk"] * 128, x_sample.shape[1])
    xs[SEGS[0]["s0"] + lo - a0: SEGS[0]["s0"] + hi - a0] = x_sample[0, lo:hi]
    vm[SEGS[0]["s0"] + lo - a0: SEGS[0]["s0"] + hi - a0] = 1.0
    j, half = c // 2, c % 2
    b0 = 1024 * half - HB * 128
    lo, hi = max(b0, 0), min(b0 + SEGS[1]["nblk"] * 128, x_prompt.shape[1])
    xs[SEGS[1]["s0"] + lo - b0: SEGS[1]["s0"] + hi - b0] = x_prompt[j, lo:hi]
    vm[SEGS[1]["s0"] + lo - b0: SEGS[1]["s0"] + hi - b0] = 1.0
    kb = np.where(vm[:, 0] > 0, 0.0, -1.0e6).astype(np.float32).reshape(1, NTOK)
    return xs, vm, kb


def make_in_maps(inp):
    f32 = lambda a: np.ascontiguousarray(np.asarray(a, dtype=np.float32))
    inp = {k: f32(v) for k, v in inp.items()}
    ws = _build_wsrc(inp)
    base, wband, ident = _consts()
    cp = np.empty((4, 128, 88, 4), np.float32)
    for i in range(4):
        cw = inp["ffn_conv_w"][i]
        cb = inp["ffn_conv_b"][i]
        for kk in range(3):
            cp[i, :, :, kk] = cw[kk].reshape(88, 128).T
        cp[i, :, :, 3] = cb.reshape(88, 128).T
    cp = cp.reshape(4, 128, 88 * 4)
    common = {"wsrc": ws, "gmix": inp["norm_mix"], "gffn": inp["norm_ffn"], "gfin": inp["norm_final"].reshape(1, D),
              "pscale": inp["pool_scale"], "sink": inp["attn_sink"], "cp": cp, "base": base, "wband": wband, "ident": ident}
    maps = []
    for c in range(NCORES):
        xs, vm, kb = _core_stream(c, inp["x_prompt"], inp["x_sample"])
        m = dict(common)
        m.update({"x_in": xs, "vmask": vm, "kmask": kb})
        maps.append(m)
    return maps


_NC_CACHE = {}


def kernel(**inputs):
    if "nc" not in _NC_CACHE:
        _NC_CACHE["nc"] = build_program()
    nc = _NC_CACHE["nc"]
    maps = make_in_maps(inputs)
    res = run_bass_kernel_spmd(nc, maps, core_ids=list(range(NCORES)))
    y_prompt = np.empty((4, 2048, D), np.float32)
    y_sample = np.empty((1, 16384, D), np.float32)
    for c in range(NCORES):
        y = np.asarray(res.results[c]["y"])
        y_sample[0, 2048 * c:2048 * (c + 1)] = y[:2048]
        y_prompt[c // 2, 1024 * (c % 2):1024 * (c % 2 + 1)] = y[2048:]
    return (y_prompt, y_sample)
```
